# Optimizing a Trainium2 kernel written in Bass

```python
import math
import jax, jax.numpy as jnp
from jax import lax
import numpy as np

D_MODEL = 1024
BATCH = 8
SEQ = 2048
DEPTH = 1

D_FF = 2816
F_GROUPS = 4
F_GROUP_CH = 128
F_WIDTH = F_GROUPS * F_GROUP_CH
DA_HEADS = 4
DA_HEAD_DIM = 64
DA_V_DIM = 2 * DA_HEAD_DIM
QK_WIDTH = DA_HEADS * 2 * DA_HEAD_DIM
V_WIDTH = DA_HEADS * DA_V_DIM
GATE_WIDTH = 2 * D_MODEL
IN_WIDTH = F_WIDTH + 2 * QK_WIDTH + V_WIDTH + GATE_WIDTH
REL_BUCKETS = 32
REL_MAX_DIST = 128
Q_BLOCK = 128
NORM_EPS = 1e-6
SUBLN_EPS = 1e-5

kernel_name = "hybrid_fnet_diffattn_macaron_encoder"


def rms_norm(x, g, eps=NORM_EPS):
    xf = x.astype(jnp.float32)
    y = xf * lax.rsqrt(jnp.mean(xf * xf, axis=-1, keepdims=True) + eps)
    return (y * g.astype(jnp.float32)).astype(x.dtype)


def swiglu(x, wg, wu, wd):
    return (jax.nn.silu(x @ wg) * (x @ wu)) @ wd


def t5_bidirectional_bucket(rel):
    nb = REL_BUCKETS // 2
    max_exact = nb // 2
    ret = (rel > 0).astype(jnp.int32) * nb
    n = jnp.abs(rel)
    nf = jnp.maximum(n, 1).astype(jnp.float32)
    large = max_exact + (jnp.log(nf / max_exact) / math.log(REL_MAX_DIST / max_exact)
                         * (nb - max_exact)).astype(jnp.int32)
    large = jnp.minimum(large, nb - 1)
    return ret + jnp.where(n < max_exact, n, large)


def fourier_branch(u_f):
    B, S, _ = u_f.shape
    uf = u_f.reshape(B, S, F_GROUPS, F_GROUP_CH).astype(jnp.float32)
    y = jnp.fft.fft2(uf, axes=(1, 3), norm="ortho").real
    return y.astype(u_f.dtype).reshape(B, S, F_WIDTH)


def diff_attention(q, k, v, positions, rel_bias, lam):
    B, S = q.shape[0], q.shape[1]
    nb = S // Q_BLOCK
    scale = DA_HEAD_DIM ** -0.5
    q = jnp.transpose(q, (3, 0, 2, 1, 4))
    k = jnp.transpose(k, (3, 0, 2, 1, 4))
    vh = jnp.transpose(v, (0, 2, 1, 3))
    qb = q.reshape(2, B, DA_HEADS, nb, Q_BLOCK, DA_HEAD_DIM)
    qb = jnp.moveaxis(qb, 3, 0)
    pb = jnp.moveaxis(positions.reshape(B, nb, Q_BLOCK), 1, 0)

    def block(args):
        qblk, pblk = args
        rel = positions[:, None, :] - pblk[:, :, None]
        bias = rel_bias[t5_bidirectional_bucket(rel)]
        bias = jnp.transpose(bias, (0, 3, 1, 2)).astype(jnp.float32)
        s1 = jnp.einsum('bhqd,bhkd->bhqk', qblk[0], k[0]).astype(jnp.float32) * scale + bias
        s2 = jnp.einsum('bhqd,bhkd->bhqk', qblk[1], k[1]).astype(jnp.float32) * scale + bias
        attn = jax.nn.softmax(s1, axis=-1) - lam * jax.nn.softmax(s2, axis=-1)
        return jnp.einsum('bhqk,bhkv->bhqv', attn.astype(vh.dtype), vh)

    out = lax.map(block, (qb, pb))
    out = jnp.transpose(out, (1, 0, 3, 2, 4))
    return out.reshape(B, S, DA_HEADS, DA_V_DIM)


def hybrid_mixer(h, positions, rel_bias, w_in, lq1, lk1, lq2, lk2, subln_g,
                 w_fourier_out, w_attn_out, w_out, layer_idx):
    B, S, _ = h.shape
    proj = h @ w_in
    cuts = np.cumsum([F_WIDTH, QK_WIDTH, QK_WIDTH, V_WIDTH, D_MODEL]).tolist()
    u_f, q, k, v, g_a, g_b = jnp.split(proj, cuts, axis=-1)

    y_a = fourier_branch(u_f) @ w_fourier_out

    lambda_init = 0.8 - 0.6 * math.exp(-0.3 * layer_idx)
    lam = (jnp.exp(jnp.sum(lq1.astype(jnp.float32) * lk1.astype(jnp.float32)))
           - jnp.exp(jnp.sum(lq2.astype(jnp.float32) * lk2.astype(jnp.float32)))
           + lambda_init)
    q = q.reshape(B, S, DA_HEADS, 2, DA_HEAD_DIM)
    k = k.reshape(B, S, DA_HEADS, 2, DA_HEAD_DIM)
    v = v.reshape(B, S, DA_HEADS, DA_V_DIM)
    o = diff_attention(q, k, v, positions, rel_bias, lam)
    o = rms_norm(o, subln_g, SUBLN_EPS) * (1.0 - lambda_init)
    y_b = o.reshape(B, S, V_WIDTH) @ w_attn_out

    merged = jax.nn.sigmoid(g_a) * y_a + jax.nn.sigmoid(g_b) * y_b
    return merged @ w_out


def setup_inputs(seed: int = 0) -> dict:
    key = jax.random.key(seed)
    ks = jax.random.split(key, 24)
    L, D = DEPTH, D_MODEL
    nrm = lambda k, shape, fan_in: jax.random.normal(k, shape, jnp.float32) * fan_in ** -0.5
    gain = lambda k, shape: 1.0 + 0.02 * jax.random.normal(k, shape, jnp.float32)
    return {
        "x": jax.random.normal(ks[0], (BATCH, SEQ, D), jnp.float32),
        "positions": jnp.broadcast_to(jnp.arange(SEQ, dtype=jnp.int32), (BATCH, SEQ)),
        "rel_bias": 0.1 * jax.random.normal(ks[1], (REL_BUCKETS, DA_HEADS), jnp.float32),
        "ffn1_norm": gain(ks[2], (L, D)),
        "ffn1_wg": nrm(ks[3], (L, D, D_FF), D),
        "ffn1_wu": nrm(ks[4], (L, D, D_FF), D),
        "ffn1_wd": nrm(ks[5], (L, D_FF, D), D_FF),
        "mix_norm": gain(ks[6], (L, D)),
        "w_in": nrm(ks[7], (L, D, IN_WIDTH), D),
        "lambda_q1": 0.1 * jax.random.normal(ks[8], (L, DA_HEAD_DIM), jnp.float32),
        "lambda_k1": 0.1 * jax.random.normal(ks[9], (L, DA_HEAD_DIM), jnp.float32),
        "lambda_q2": 0.1 * jax.random.normal(ks[10], (L, DA_HEAD_DIM), jnp.float32),
        "lambda_k2": 0.1 * jax.random.normal(ks[11], (L, DA_HEAD_DIM), jnp.float32),
        "subln_g": gain(ks[12], (L, DA_V_DIM)),
        "w_fourier_out": nrm(ks[13], (L, F_WIDTH, D), F_WIDTH),
        "w_attn_out": nrm(ks[14], (L, V_WIDTH, D), V_WIDTH),
        "w_out": nrm(ks[15], (L, D, D), D),
        "ffn2_norm": gain(ks[16], (L, D)),
        "ffn2_wg": nrm(ks[17], (L, D, D_FF), D),
        "ffn2_wu": nrm(ks[18], (L, D, D_FF), D),
        "ffn2_wd": nrm(ks[19], (L, D_FF, D), D_FF),
        "final_norm": gain(ks[20], (D,)),
    }


def reference(x, positions, rel_bias, ffn1_norm, ffn1_wg, ffn1_wu, ffn1_wd,
              mix_norm, w_in, lambda_q1, lambda_k1, lambda_q2, lambda_k2, subln_g,
              w_fourier_out, w_attn_out, w_out, ffn2_norm, ffn2_wg, ffn2_wu,
              ffn2_wd, final_norm):
    for l in range(DEPTH):
        x = x + 0.5 * swiglu(rms_norm(x, ffn1_norm[l]), ffn1_wg[l], ffn1_wu[l], ffn1_wd[l])
        x = x + hybrid_mixer(rms_norm(x, mix_norm[l]), positions, rel_bias, w_in[l],
                             lambda_q1[l], lambda_k1[l], lambda_q2[l], lambda_k2[l],
                             subln_g[l], w_fourier_out[l], w_attn_out[l], w_out[l], l)
        x = x + 0.5 * swiglu(rms_norm(x, ffn2_norm[l]), ffn2_wg[l], ffn2_wu[l], ffn2_wd[l])
    return rms_norm(x, final_norm)
```

```python
import bisect
import math
from contextlib import ExitStack

import numpy as np
import ml_dtypes

import concourse.bass as bass
import concourse.mybir as mybir
from concourse.bass_utils import run_bass_kernel_spmd

F32 = mybir.dt.float32
BF16 = mybir.dt.bfloat16
AF = mybir.ActivationFunctionType
ALU = mybir.AluOpType
AX = mybir.AxisListType

S = 2048
D = 1024
DFF = 2816
NF = 22
NT = 4
NORM_EPS = 1e-6
SUBLN_EPS = 1e-5
LAMBDA_INIT = 0.8 - 0.6 * math.exp(-0.3 * 0)
T5_THRESH = [1, 2, 3, 4, 5, 6, 7, 8, 12, 16, 23, 32, 46, 64, 91]

PV_G1, PV_GM, PV_G2, PV_GF, PV_SUB, PV_RBLO, PV_RBHI, PV_LAM, PV_N = 0, 8, 16, 24, 32, 33, 37, 48, 304


class TL:
    __slots__ = ("ap", "space", "s", "e")

    def __init__(self, ap, space, s, e):
        self.ap, self.space, self.s, self.e = ap, space, s, e


class Buf:
    def __init__(self, space, ap2d, off, esz, A, B):
        self.space, self.off, self.esz, self.A, self.B = space, off, esz, A, B
        self.ap3 = ap2d.rearrange("p (a b) -> p a b", a=A)

    def t(self, a=None, b=None, p=None):
        A, B = self.A, self.B
        if a is None:
            a0, a1, ai = 0, A, slice(0, A)
        elif isinstance(a, tuple):
            a0, a1 = a
            ai = slice(a0, a1)
        else:
            a0, a1, ai = a, a + 1, a
        b0, b1 = (0, B) if b is None else b
        ps = slice(0, 128) if p is None else slice(p[0], p[1])
        ap = self.ap3[ps, ai, b0:b1]
        s = self.off + (a0 * B + b0) * self.esz
        e = self.off + ((a1 - 1) * B + b1) * self.esz
        return TL(ap, self.space, s, e)


class Op:
    __slots__ = ("eng", "fn", "raw", "oth", "idx", "signal", "dkey", "dcum", "tick")


class Sched:
    def __init__(self):
        self.ops = []
        self.rec = {}
        self.dcnt = {}

    def _overlaps(self, space, s, e):
        if space not in self.rec:
            self.rec[space] = []
        recs = self.rec[space]
        i = bisect.bisect_right([r[0] for r in recs], s) - 1 if len(recs) > 64 else 0
        if i < 0:
            i = 0
        out = []
        for j in range(i, len(recs)):
            r = recs[j]
            if r[0] >= e:
                break
            if r[1] > s:
                out.append(j)
        return recs, out

    def add(self, eng, fn, reads=(), writes=(), dkey=None):
        op = Op()
        op.eng, op.fn, op.idx, op.signal, op.dkey, op.tick = eng, fn, len(self.ops), False, dkey, None
        if dkey is not None:
            self.dcnt[dkey] = self.dcnt.get(dkey, 0) + 16
            op.dcum = self.dcnt[dkey]
            tok = ("d", dkey, op.dcum)
        else:
            op.dcum = None
            tok = ("e", eng, op.idx)
        raw, oth = {}, {}

        def note(d, t):
            k = (t[0], t[1])
            if d.get(k, -1) < t[2]:
                d[k] = t[2]

        for tl in reads:
            recs, idxs = self._overlaps(tl.space, tl.s, tl.e)
            for j in idxs:
                r = recs[j]
                if r[2] is not None:
                    note(raw, r[2])
                note(r[3], tok)
        for tl in writes:
            recs, idxs = self._overlaps(tl.space, tl.s, tl.e)
            new = []
            for j in idxs:
                r = recs[j]
                if r[2] is not None:
                    note(oth, r[2])
                for k, v in r[3].items():
                    note(oth, (k[0], k[1], v))
                if r[0] < tl.s:
                    new.append([r[0], tl.s, r[2], dict(r[3])])
                if r[1] > tl.e:
                    new.append([tl.e, r[1], r[2], dict(r[3])])
            for j in reversed(idxs):
                del recs[j]
            new.append([tl.s, tl.e, tok, {}])
            recs.extend(new)
            recs.sort(key=lambda r: r[0])
        me = (tok[0], tok[1])
        for d in (raw, oth):
            if me in d and d[me] == tok[2]:
                del d[me]
        op.raw, op.oth = raw, oth
        self.ops.append(op)
        return op

    def finalize(self):
        for op in self.ops:
            keep = {}
            for d, is_raw in ((op.raw, True), (op.oth, False)):
                for (kind, name), val in d.items():
                    if kind == "e" and name == op.eng and op.eng == "pe":
                        continue
                    k = (kind, name)
                    if keep.get(k, -1) < val:
                        keep[k] = val
            op.raw = keep
            for (kind, name), val in keep.items():
                if kind == "e":
                    self.ops[val].signal = True
        cnt = {}
        for op in self.ops:
            if op.dkey is None and op.signal:
                cnt[op.eng] = cnt.get(op.eng, 0) + 1
                op.tick = cnt[op.eng]
        self.nsig = cnt

    def emit(self, eng, e, esem, dsem):
        waited = {}
        n = 0
        for op in self.ops:
            if op.eng != eng:
                continue
            for (kind, name), val in op.raw.items():
                if kind == "e":
                    sem, v = esem[name], self.ops[val].tick
                else:
                    sem, v = dsem[name], val
                k = (kind, name)
                if waited.get(k, -1) >= v:
                    continue
                waited[k] = v
                e.wait_ge(sem, v)
            ins = op.fn(e)
            if op.dkey is not None:
                ins.then_inc(dsem[op.dkey], 16)
            elif op.signal:
                ins.then_inc(esem[eng], 1)
            n += 1
        return n


def build_program(stop_after=None, taps=()):
    nc = bass.Bass("TRN2", target_bir_lowering=False)
    sc = Sched()
    taps = set(taps)

    def dram(name, shape, dt, kind="ExternalInput"):
        return nc.dram_tensor(name, shape, dt, kind=kind)

    xT_d = dram("xT", [D, S], F32).ap()
    wgu_d = [dram("wgu%d" % i, [NF, 128, 2 * 8 * 128], F32).ap() for i in (1, 2)]
    wd_d = [dram("wd%d" % i, [8, 128, NF * 128], F32).ap() for i in (1, 2)]
    wuf_d = dram("wuf", [4, 128, 8 * 128], F32).ap()
    wv_d = dram("wv", [128, 8 * 512], F32).ap()
    wqk_d = dram("wqk", [4, 128, 2 * 8 * 128], F32).ap()
    wgate_d = dram("wgate", [8, 128, 2 * 8 * 128], F32).ap()
    wfa_d = dram("wfa", [8, 128, 2 * 4 * 128], F32).ap()
    wout_d = dram("wout", [8, 128, 8 * 128], F32).ap()
    pvec_d = dram("pvec", [128, PV_N], F32).ap()
    rbrep_d = dram("rbrep", [32, 4 * 128], F32).ap()
    cc_d = dram("cc", [128, 256], BF16).ap()
    slabs_d = dram("slabs", [4, 2, 128, 16 * 512], BF16).ap()
    outT_d = dram("outT", [D, S], F32, kind="ExternalOutput").ap()
    tscr_h = dram("tscr", [4, 128 * 512], BF16, kind="Internal")
    tscr_d = tscr_h.ap()
    tap_d = {}

    es = ExitStack()
    with es:
        Xt = es.enter_context(nc.sbuf_tensor("X", [128, 8 * S], F32))
        Ht = es.enter_context(nc.sbuf_tensor("H", [128, 8 * S], BF16))
        Ct = es.enter_context(nc.sbuf_tensor("CST", [128, 8192], BF16))
        SCR_B = 94208
        St = es.enter_context(nc.sbuf_tensor("SCR", [128, SCR_B // 2], BF16))
        PS = [es.enter_context(nc.psum_tensor("ps%d" % i, [128, 512], F32)) for i in range(8)]
        psT = [TL(PS[i][:], "ps%d" % i, 0, 1) for i in range(8)]

        def pst(i, c0=0, c1=512, p=None):
            ap = PS[i][:, c0:c1] if p is None else PS[i][p[0]:p[1], c0:c1]
            return TL(ap, "ps%d" % i, 0, 1)

        X_OFF, H_OFF, C_OFF, S_OFF = 0, 1 << 20, 2 << 20, 3 << 20
        X = Buf("sb", Xt[:], X_OFF, 4, 8, S)
        H = Buf("sb", Ht[:], H_OFF, 2, 8, S)

        def cbuf(off, nbytes, dt, A, B):
            esz = 4 if dt == F32 else 2
            assert off % 4 == 0 and off + nbytes <= 16384 and A * B * esz == nbytes, (off, nbytes, A, B)
            ap = Ct[:, off // 2:(off + nbytes) // 2]
            if dt == F32:
                ap = ap.bitcast(F32)
            return Buf("sb", ap, C_OFF + off, esz, A, B)

        def sbuf(off, dt, A, B):
            esz = 4 if dt == F32 else 2
            nbytes = A * B * esz
            assert off % 4 == 0 and off + nbytes <= SCR_B, (off, nbytes)
            ap = St[:, off // 2:(off + nbytes) // 2]
            if dt == F32:
                ap = ap.bitcast(F32)
            return Buf("sb", ap, S_OFF + off, esz, A, B)

        PV = cbuf(0, PV_N * 4, F32, 1, PV_N)
        LAMT = cbuf(1216, 64 * 4, F32, 1, 64)
        LAMS = cbuf(1472, 8 * 4, F32, 1, 8)
        ONES1 = cbuf(1504, 256, BF16, 1, 128)
        ONESM = cbuf(1760, 256, BF16, 1, 128)
        ONESV = cbuf(2016, 256, BF16, 1, 128)
        IDENT = cbuf(2272, 256, BF16, 1, 128)
        CC = cbuf(2528, 512, BF16, 1, 256)
        BW = cbuf(3040, 4 * 384 * 2, BF16, 4, 384)
        SQ = cbuf(6112, 4 * 1024, BF16, 4, 512)
        RSTD = cbuf(11232, 2 * 2048, F32, 2, 512)
        PIDX = cbuf(15328, 4, F32, 1, 1)
        assert 15332 <= 16384

        def pv(col, n=1):
            return PV.t(0, (col, col + n))

        def mm(ps, lhsT, rhs, start, stop):
            rd = [lhsT, rhs] + ([] if start else [ps])
            sc.add("pe", lambda e: e.matmul(ps.ap, lhsT.ap, rhs.ap, start=start, stop=stop),
                   reads=rd, writes=[ps])

        def act(out, in_, func, bias=None, scale=1.0):
            rd = [in_] + ([bias] if isinstance(bias, TL) else [])
            b = bias.ap if isinstance(bias, TL) else (0.0 if bias is None else bias)
            sc.add("act", lambda e: e.activation(out=out.ap, in_=in_.ap, func=func, bias=b, scale=scale),
                   reads=rd, writes=[out])

        def vop(eng, name, out, ins, *args, **kw):
            def fn(e):
                k = dict(kw)
                for key, tl in ins:
                    k[key] = tl.ap
                return getattr(e, name)(out=out.ap, **k)
            sc.add(eng, fn, reads=[tl for _, tl in ins], writes=[out])

        def tt(eng, out, a, b, op):
            vop(eng, "tensor_tensor", out, [("in0", a), ("in1", b)], op=op)

        def stt(eng, out, in0, scalar, in1, op0, op1):
            if isinstance(scalar, TL):
                vop(eng, "scalar_tensor_tensor", out, [("in0", in0), ("scalar", scalar), ("in1", in1)], op0=op0, op1=op1)
            else:
                vop(eng, "scalar_tensor_tensor", out, [("in0", in0), ("in1", in1)], scalar=scalar, op0=op0, op1=op1)

        def ts(eng, out, in0, s1, s2, op0, op1=None):
            ins = [("in0", in0)]
            kw = {}
            if isinstance(s1, TL):
                ins.append(("scalar1", s1))
            else:
                kw["scalar1"] = s1
            if isinstance(s2, TL):
                ins.append(("scalar2", s2))
            else:
                kw["scalar2"] = s2
            kw["op0"] = op0
            if op1 is not None:
                kw["op1"] = op1
            vop(eng, "tensor_scalar", out, ins, **kw)

        def tss(eng, out, in_, scalar, op):
            if isinstance(scalar, TL):
                vop(eng, "tensor_single_scalar", out, [("in_", in_), ("scalar", scalar)], op=op)
            else:
                vop(eng, "tensor_single_scalar", out, [("in_", in_)], scalar=scalar, op=op)

        def rstd_from(rs, ps, eps):
            act(rs, ps, AF.Sqrt, bias=eps)
            vop("dve", "reciprocal", rs, [("in_", rs)])

        def cp(eng, out, in_):
            vop(eng, "tensor_copy", out, [("in_", in_)])

        def memset(eng, out, val):
            sc.add(eng, lambda e: e.memset(out.ap, val), writes=[out])

        def dma(eng, out, in_, key):
            sc.add(eng, lambda e: e.dma_start(out=out.ap, in_=in_.ap), reads=[in_], writes=[out], dkey=key)

        def dr(ap, name):
            return TL(ap, "dram:" + name, 0, 1)

        def tap(name, buf_tile, shape, dt):
            if name not in taps:
                return
            t = dram("tap_" + name, shape, dt, kind="ExternalOutput").ap()
            tap_d[name] = t
            dma("sp", TL(t, "dram:tap_" + name, 0, 1), buf_tile, "tap_" + name)

        xv = xT_d.rearrange("(c p) t -> p c t", p=128)
        dma("sp", PV.t(0), dr(pvec_d, "pvec"), "pv")
        for tb in range(NT):
            for c in range(8):
                dma("sp", X.t(c, (tb * 512, tb * 512 + 512)), dr(xv[:, c, tb * 512:tb * 512 + 512], "xT"), "x%d_%d" % (tb, c))
        dma("sp", CC.t(0), dr(cc_d, "cc"), "cc")
        memset("dve", ONES1.t(0), 1.0)
        memset("dve", ONESM.t(0), 1.0 / 1024.0)
        memset("dve", ONESV.t(0), 1.0 / 128.0)

        def norm_steps(tbs, emit_out):
            steps = []
            for tb in tbs:
                tsl = (tb * 512, tb * 512 + 512)

                def sqf(c, tsl=tsl):
                    if c < 8:
                        tt("pool", SQ.t(c % 4), X.t(c, tsl), X.t(c, tsl), ALU.mult)

                def mmf(c):
                    mm(psT[6], ONESM.t(0), SQ.t(c % 4), c == 0, c == 7)

                steps.append(lambda sqf=sqf: (sqf(0), sqf(1), sqf(2)))
                steps.append(lambda sqf=sqf: sqf(3))
                for c in range(8):
                    steps.append(lambda c=c, sqf=sqf, mmf=mmf: (mmf(c), sqf(c + 4)))
                steps.append(lambda tb=tb: rstd_from(RSTD.t(tb % 2), psT[6], NORM_EPS))
                for c0 in (0, 4):
                    steps.append(lambda tb=tb, c0=c0: [emit_out(tb, c, RSTD.t(tb % 2)) for c in range(c0, c0 + 4)])
            return steps

        def h_out(gcol):
            def f(tb, c, rs):
                tsl = (tb * 512, tb * 512 + 512)
                stt("dve", H.t(c, tsl), X.t(c, tsl), pv(gcol + c), rs, ALU.mult, ALU.mult)
            return f

        def run_all(steps):
            for st in steps:
                st()

        def make_ffn(wgu, wd, wname):
            A = sbuf(0, BF16, NF, 1024)
            WD = [sbuf(45056 + i * 5632, BF16, NF, 128) for i in range(3)]
            SG = [sbuf(61952 + i * 2048, F32, 1, 512) for i in range(2)]
            WGU = [sbuf(81920 + i * 4096, BF16, 16, 128) for i in range(3)]
            seq_gu = [(th, f) for th in range(2) for f in range(NF)]
            seq_d = [(th, dc) for th in range(2) for dc in range(8)]

            def load_gu(i):
                th, f = seq_gu[i]
                dma("pool", WGU[i % 3].t(), dr(wgu[f], wname + "gu"), "%sgu%d" % (wname, i % 3))

            def load_d(i):
                th, dc = seq_d[i]
                dma("pool", WD[i % 3].t(), dr(wd[dc], wname + "d"), "%sd%d" % (wname, i % 3))

            def prefetch():
                load_gu(0)
                load_gu(1)

            def body(pump0=(), pump1=(), hooks=None, after_gu=None):
                pumps = [list(pump0), list(pump1)]
                hooks = hooks or {}
                ev = 0
                for i, (th, f) in enumerate(seq_gu):
                    pump = pumps[th]
                    if i + 2 < len(seq_gu):
                        load_gu(i + 2)
                    if f == NF - 4:
                        load_d(th * 8)
                    if f == NF - 2:
                        load_d(th * 8 + 1)
                    w = WGU[i % 3]
                    if (th, f) in hooks:
                        hooks[(th, f)]()
                    for tb2 in range(2):
                        tb = th * 2 + tb2
                        tsl = (tb * 512, tb * 512 + 512)
                        pg, pu = (0, 1) if ev % 2 == 0 else (2, 3)
                        for k in range(8):
                            mm(psT[pg], w.t(k), H.t(k, tsl), k == 0, k == 7)
                        for k in range(8):
                            mm(psT[pu], w.t(8 + k), H.t(k, tsl), k == 0, k == 7)
                        sg = SG[ev % 2].t(0)
                        act(sg, psT[pg], AF.Silu)
                        tt("dve", A.t(f, (tb2 * 512, tb2 * 512 + 512)), sg, psT[pu], ALU.mult)
                        ev += 1
                        if pump:
                            pump.pop(0)()
                    if f == NF - 1:
                        if th == 1 and after_gu is not None:
                            after_gu()
                        for dc in range(8):
                            j = th * 8 + dc
                            if dc + 2 < 8:
                                load_d(j + 2)
                            wdt = WD[j % 3]
                            for tb2 in range(2):
                                tb = th * 2 + tb2
                                tsl = (tb * 512, tb * 512 + 512)
                                py = 4 + (dc * 2 + tb2) % 2
                                for ff in range(NF):
                                    mm(psT[py], wdt.t(ff), A.t(ff, (tb2 * 512, tb2 * 512 + 512)), ff == 0, ff == NF - 1)
                                stt("dve", X.t(dc, tsl), psT[py], 0.5, X.t(dc, tsl), ALU.mult, ALU.add)
                                if pump:
                                    pump.pop(0)()
                        run_all(pump)
                        del pump[:]
            return prefetch, body

        OUT = [sbuf(66048 + i * 2048, F32, 1, 512) for i in range(4)]
        ov = outT_d.rearrange("(c p) t -> p c t", p=128)
        out_n = [0]

        def f_out(tb, c, rs):
            tsl = (tb * 512, tb * 512 + 512)
            o = OUT[out_n[0] % 4].t(0)
            stt("dve", o, X.t(c, tsl), pv(PV_GF + c), rs, ALU.mult, ALU.mult)
            dma("sp", TL(ov[:, c, tsl[0]:tsl[1]], "dram:outT%d_%d" % (c, tb), 0, 1), o, "out%d" % (out_n[0] % 4))
            out_n[0] += 1

        def attn_setup_steps():
            TT = sbuf(36864, BF16, 4, 512)
            RV = sbuf(66048, F32, 1, 512)
            NV = sbuf(68096, F32, 1, 512)
            CNT = sbuf(70144, F32, 1, 512)
            RBF = sbuf(72192, F32, 1, 512)
            OH = sbuf(74240, BF16, 1, 512)
            RBH = sbuf(75264, BF16, 1, 512)
            IDF = sbuf(77312, F32, 1, 128)
            P32 = (0, 32)
            rv, nv, cnt = RV.t(0, p=P32), NV.t(0, p=P32), CNT.t(0, p=P32)
            p1 = []

            def lam():
                for j in range(2):
                    tt("dve", LAMT.t(0), pv(PV_LAM + 128 * j, 64), pv(PV_LAM + 128 * j + 64, 64), ALU.mult)
                    vop("dve", "tensor_reduce", LAMS.t(0, (j, j + 1)), [("in_", LAMT.t(0))], axis=AX.X, op=ALU.add)
            p1.append(lam)

            def lam2():
                act(LAMS.t(0, (2, 4)), LAMS.t(0, (0, 2)), AF.Exp)
                ts("dve", LAMS.t(0, (4, 5)), LAMS.t(0, (2, 3)), LAMS.t(0, (3, 4)), LAMBDA_INIT, ALU.subtract, ALU.add)
                tss("dve", LAMS.t(0, (5, 6)), LAMS.t(0, (4, 5)), -1.0, ALU.mult)
                tss("dve", LAMS.t(0, (6, 7)), pv(PV_SUB), 1.0 - LAMBDA_INIT, ALU.mult)
            p1.append(lam2)

            def idx():
                sc.add("pool", lambda e: e.iota(PIDX.t(0).ap, [[0, 1]], base=0, channel_multiplier=1,
                                                allow_small_or_imprecise_dtypes=True), writes=[PIDX.t(0)])
                sc.add("pool", lambda e: e.iota(IDF.t(0).ap, [[1, 128]], base=0, channel_multiplier=0,
                                                allow_small_or_imprecise_dtypes=True), writes=[IDF.t(0)])
                sc.add("pool", lambda e: e.iota(rv.ap, [[-1, 512]], base=256, channel_multiplier=0,
                                                allow_small_or_imprecise_dtypes=True), writes=[rv])
                dma("sp", RBF.t(0, p=P32), dr(rbrep_d, "rbrep"), "rbrep")
            p1.append(idx)

            def idn():
                tss("dve", IDENT.t(0), IDF.t(0), PIDX.t(0), ALU.is_equal)
                tt("dve", nv, rv, rv, ALU.mult)
                tss("dve", cnt, nv, float(T5_THRESH[0] ** 2), ALU.is_ge)
            p1.append(idn)
            for i in range(1, len(T5_THRESH), 2):
                def thr(i=i):
                    for t in T5_THRESH[i:i + 2]:
                        stt("dve", cnt, nv, float(t * t), cnt, ALU.is_ge, ALU.add)
                p1.append(thr)

            def fin():
                ts("dve", rv, rv, 0.0, 16.0, ALU.is_gt, ALU.mult)
                tt("dve", cnt, cnt, rv, ALU.add)
                tss("dve", OH.t(0, p=P32), cnt, PIDX.t(0, p=P32), ALU.is_equal)
                cp("dve", RBH.t(0, p=P32), RBF.t(0, p=P32))
            p1.append(fin)
            p2 = []
            for h in range(4):
                def tabh(h=h):
                    mm(psT[7], RBH.t(0, (h * 128, h * 128 + 128), p=P32), OH.t(0, p=P32), True, True)
                    cp("dve", TT.t(h), psT[7])
                p2.append(tabh)

            def toep():
                dma("sp", TL(tscr_d.rearrange("j (p m) -> p j m", p=128), "dram:tscr", 0, 1), TT.t(), "tscr_w")
                win = bass.AP(tscr_h, 128, [[511, 128], [65536, 4], [1, 384]])
                dma("sp", BW.t(), TL(win, "dram:tscr", 0, 1), "tscr_r")
            p2.append(toep)
            return p1, p2

        evac_rr = [0]

        def evac(out, ps, scale=None):
            evac_rr[0] += 1
            if scale is not None:
                act(out, ps, AF.Copy, scale=scale)
            elif evac_rr[0] % 2 == 0:
                act(out, ps, AF.Copy)
            else:
                cp("dve", out, ps)

        def mixer(after_merge=None):
            FB = sbuf(0, BF16, 4, S)
            ACAS = sbuf(16384, BF16, 16, 1024)
            SLAB = [sbuf(49152 + i * 16384, BF16, 16, 512) for i in range(2)]
            WUF = [sbuf(81920 + i * 2048, BF16, 8, 128) for i in range(2)]
            bank = [0]

            def nb(n=4):
                bank[0] = (bank[0] + 1) % n
                return bank[0]

            for g in range(4):
                if g >= 2:
                    dma("pool", WUF[g % 2].t(), dr(wuf_d[g], "wuf"), "wuf%d" % (g % 2))
                for tb in range(NT):
                    tsl = (tb * 512, tb * 512 + 512)
                    b = nb()
                    for k in range(8):
                        mm(psT[b], WUF[g % 2].t(k), H.t(k, tsl), k == 0, k == 7)
                    evac(FB.t(g, tsl), psT[b])
            tap("uf", FB.t(), [128, 4, S], BF16)
            for s_c in range(16):
                ssl = (s_c * 128, s_c * 128 + 128)
                for gp in range(2):
                    b = nb()
                    for gi in range(2):
                        g = gp * 2 + gi
                        mm(pst(b, gi * 256, gi * 256 + 256), FB.t(g, ssl), CC.t(0), True, True)
                    evac(ACAS.t(s_c, (gp * 512, gp * 512 + 512)), psT[b])
            def load_slab(i):
                sb_, trig = divmod(i, 2)
                dma("sp", SLAB[trig].t(), dr(slabs_d[sb_, trig], "slabs"), "slab%d" % trig)
            load_slab(0)
            load_slab(1)
            for sb_ in range(4):
                for trig in range(2):
                    for g in range(4):
                        for s_c in range(16):
                            mm(psT[g], ACAS.t(s_c, (g * 256 + trig * 128, g * 256 + trig * 128 + 128)),
                               SLAB[trig].t(s_c), trig == 0 and s_c == 0, trig == 1 and s_c == 15)
                    if sb_ < 3:
                        load_slab((sb_ + 1) * 2 + trig)
                for g in range(4):
                    evac(FB.t(g, (sb_ * 512, sb_ * 512 + 512)), psT[g])
            tap("yf", FB.t(), [128, 4, S], BF16)
            if stop_after == "m1":
                return

            VALL = sbuf(16384, BF16, 16, 512)
            OB = sbuf(32768, BF16, 4, S)
            QB = [sbuf(49152 + i * 4096, BF16, 1, S) for i in range(2)]
            KP = [[sbuf(57344 + i * 8192 + j * 4096, BF16, 1, S) for j in range(2)] for i in range(2)]
            WQK = [sbuf(73728 + i * 4096, BF16, 16, 128) for i in range(2)]
            WV = sbuf(81920, BF16, 8, 512)
            PT = [sbuf(81920 + i * 1024, BF16, 1, 512) for i in range(4)]
            OT = [sbuf(86016 + i * 2048, F32, 1, 512) for i in range(4)]
            dma("pool", WV.t(), dr(wv_d, "wv"), "wv")
            dma("pool", WQK[0].t(), dr(wqk_d[0], "wqk"), "wqk0")
            for i in range(2):
                memset("dve", KP[i][0].t(0, p=(64, 128)), 0.0)
                memset("dve", KP[i][1].t(0, p=(0, 64)), 0.0)
            for s_c in range(16):
                b = nb()
                for k in range(8):
                    mm(psT[b], H.t(k, (s_c * 128, s_c * 128 + 128)), WV.t(k), k == 0, k == 7)
                evac(VALL.t(s_c), psT[b])
            deferred = []

            def flush_deferred(bank):
                while deferred:
                    deferred.pop(0)(bank)

            WG0 = sbuf(73728, BF16, 16, 128)
            WFA0 = sbuf(57344, BF16, 8, 128)
            for h in range(4):
                hs = h % 2
                if h + 1 < 4:
                    dma("pool", WQK[(h + 1) % 2].t(), dr(wqk_d[h + 1], "wqk"), "wqk%d" % ((h + 1) % 2))
                elif stop_after != "m2":
                    dma("pool", WG0.t(), dr(wgate_d[0], "wgate"), "wg_first")
                    dma("pool", WFA0.t(), dr(wfa_d[0], "wfa"), "wfa_first")
                for tb in range(NT):
                    tsl = (tb * 512, tb * 512 + 512)
                    b = nb()
                    for k in range(8):
                        mm(psT[b], WQK[hs].t(k), H.t(k, tsl), k == 0, k == 7)
                    evac(QB[hs].t(0, tsl), psT[b], scale=0.125)
                    b = nb()
                    for k in range(8):
                        mm(psT[b], WQK[hs].t(8 + k), H.t(k, tsl), k == 0, k == 7)
                    cp("dve", KP[hs][0].t(0, tsl, p=(0, 64)), pst(b, p=(0, 64)))
                    act(KP[hs][1].t(0, tsl, p=(64, 128)), pst(b, p=(64, 128)), AF.Copy)
                def scores(qb, kc):
                    q0 = qb * 512
                    pair = (0, 1) if kc % 2 == 0 else (2, 3)
                    ksl = (kc * 128, kc * 128 + 128)
                    b0 = max(128 * (kc - 1), q0)
                    b1 = min(128 * (kc + 2), q0 + 512)
                    for m in range(2):
                        has_band = b1 > b0
                        mm(psT[pair[m]], KP[hs][m].t(0, ksl), QB[hs].t(0, (q0, q0 + 512)), True, not has_band)
                        if has_band:
                            mm(pst(pair[m], b0 - q0, b1 - q0), IDENT.t(0),
                               BW.t(h, (b0 - 128 * (kc - 1), b1 - 128 * (kc - 1))), False, True)
                    return pair, b0, b1

                def exps(qb, kc, pair, b0, b1):
                    q0 = qb * 512
                    for m in range(2):
                        P = PT[(kc % 2) * 2 + m]
                        segs = []
                        if b1 > b0:
                            if b0 > q0:
                                segs.append((q0, b0, pv(PV_RBHI + h)))
                            segs.append((b0, b1, None))
                            if b1 < q0 + 512:
                                segs.append((b1, q0 + 512, pv(PV_RBLO + h)))
                        elif 128 * (kc - 1) >= q0 + 512:
                            segs.append((q0, q0 + 512, pv(PV_RBHI + h)))
                        else:
                            segs.append((q0, q0 + 512, pv(PV_RBLO + h)))
                        for (c0, c1, bias) in segs:
                            act(P.t(0, (c0 - q0, c1 - q0)), pst(pair[m], c0 - q0, c1 - q0), AF.Exp, bias=bias)

                def av(kc):
                    for m in range(2):
                        P = PT[(kc % 2) * 2 + m].t(0)
                        mm(psT[4 + m], VALL.t(kc, (h * 128, h * 128 + 128)), P, kc == 0, kc == 15)
                        mm(psT[6 + m], ONES1.t(0), P, kc == 0, kc == 15)

                def end_of_block(qb):
                    qsl = (qb * 512, qb * 512 + 512)
                    cp("dve", OT[1].t(0), psT[4])
                    cp("dve", OT[3].t(0), psT[5])
                    vop("dve", "reciprocal", OT[0].t(0), [("in_", psT[6])])
                    vop("dve", "reciprocal", OT[2].t(0), [("in_", psT[7])])
                    tt("dve", OT[1].t(0), OT[1].t(0), OT[0].t(0), ALU.mult)
                    stt("dve", OT[3].t(0), OT[3].t(0), LAMS.t(0, (5, 6)), OT[2].t(0), ALU.mult, ALU.mult)
                    tt("pool", OT[1].t(0), OT[1].t(0), OT[3].t(0), ALU.add)
                    tt("pool", SQ.t(0), OT[1].t(0), OT[1].t(0), ALU.mult)

                    def subln(bank, h=h, qsl=qsl):
                        mm(psT[bank], ONESV.t(0), SQ.t(0), True, True)
                        act(RSTD.t(0), psT[bank], AF.Ln, bias=SUBLN_EPS)
                        act(RSTD.t(0), RSTD.t(0), AF.Exp, scale=-0.5)
                        stt("dve", OB.t(h, qsl), OT[1].t(0), LAMS.t(0, (6, 7)), RSTD.t(0), ALU.mult, ALU.mult)
                    deferred.append(subln)

                seq = [(qb, kc) for qb in range(NT) for kc in range(16)]
                info = scores(*seq[0])
                for i, (qb, kc) in enumerate(seq):
                    nxt = scores(*seq[i + 1]) if i + 1 < len(seq) else None
                    exps(qb, kc, *info)
                    av(kc)
                    info = nxt
                    if kc == 7:
                        flush_deferred(2)
                    if kc == 15:
                        end_of_block(qb)
            if stop_after == "m2":
                flush_deferred(2)
            tap("o", OB.t(), [128, 4, S], BF16)
            if stop_after == "m2":
                return

            MRG = sbuf(49152, BF16, 8, S)
            WG = [sbuf(16384 + i * 8192, BF16, 16, 128) for i in range(2)]
            WFA = [sbuf(16384 + 4096 + i * 8192, BF16, 8, 128) for i in range(2)]
            SGA = [sbuf(81920 + i * 2048, F32, 1, 512) for i in range(2)]
            SGB = [sbuf(86016 + i * 4096, F32, 1, 512) for i in range(2)]

            def load_m(dc):
                if dc == 0:
                    return
                dma("pool", WG[dc % 2].t(), dr(wgate_d[dc], "wgate"), "wg%d" % (dc % 2))
                dma("pool", WFA[dc % 2].t(), dr(wfa_d[dc], "wfa"), "wfa%d" % (dc % 2))
            WOR = [sbuf(16384 + dc * 2048, BF16, 8, 128) for dc in range(8)]
            load_m(0)
            it = 0
            for dc in range(8):
                if dc + 1 < 8:
                    load_m(dc + 1)
                else:
                    for d2 in range(4):
                        dma("pool", WOR[d2].t(), dr(wout_d[d2], "wout"), "wo%d" % d2)
                for tb in range(NT):
                    tsl = (tb * 512, tb * 512 + 512)
                    bs = (0, 1, 2, 3) if it % 2 == 0 else (4, 5, 6, 7)
                    wg_ = WG0 if dc == 0 else WG[dc % 2]
                    wfa_ = WFA0 if dc == 0 else WFA[dc % 2]
                    for k in range(8):
                        mm(psT[bs[2]], wg_.t(k), H.t(k, tsl), k == 0, k == 7)
                    for k in range(8):
                        mm(psT[bs[3]], wg_.t(8 + k), H.t(k, tsl), k == 0, k == 7)
                    for k in range(4):
                        mm(psT[bs[0]], wfa_.t(k), FB.t(k, tsl), k == 0, k == 3)
                    for k in range(4):
                        mm(psT[bs[1]], wfa_.t(4 + k), OB.t(k, tsl), k == 0, k == 3)
                    act(SGA[it % 2].t(0), psT[bs[2]], AF.Sigmoid)
                    act(SGB[it % 2].t(0), psT[bs[3]], AF.Sigmoid)
                    tt("dve", SGA[it % 2].t(0), SGA[it % 2].t(0), psT[bs[0]], ALU.mult)
                    tt("dve", SGB[it % 2].t(0), SGB[it % 2].t(0), psT[bs[1]], ALU.mult)
                    tt("pool", MRG.t(dc, tsl), SGA[it % 2].t(0), SGB[it % 2].t(0), ALU.add)
                    it += 1
                    if it == 2:
                        flush_deferred(0)
            tap("mrg", MRG.t(), [128, 8, S], BF16)
            for dc in range(4, 8):
                dma("pool", WOR[dc].t(), dr(wout_d[dc], "wout"), "wo%d" % dc)
            if after_merge is not None:
                after_merge()
            pend = []
            for tb in range(NT):
                tsl = (tb * 512, tb * 512 + 512)
                for dc in range(8):
                    for _ in range(2):
                        if pend:
                            pend.pop(0)()
                    b = nb()
                    for k in range(8):
                        mm(psT[b], WOR[dc].t(k), MRG.t(k, tsl), k == 0, k == 7)
                    tt("dve", X.t(dc, tsl), psT[b], X.t(dc, tsl), ALU.add)
                pend.extend(norm_steps([tb], h_out(PV_G2)))
            run_all(pend)

        STAGES = ["ffn1", "m1", "m2", "m3", "ffn2"]
        last = STAGES.index(stop_after) if stop_after else len(STAGES) - 1
        f1_pre, f1_body = make_ffn(wgu_d[0], wd_d[0], "f1")
        f2_pre, f2_body = make_ffn(wgu_d[1], wd_d[1], "f2")
        f1_pre()
        run_all(norm_steps([0, 1], h_out(PV_G1)))
        if last >= 1:
            as1, as2 = attn_setup_steps()

            def wuf_prefetch():
                WUF = [sbuf(81920 + i * 2048, BF16, 8, 128) for i in range(2)]
                for g in range(2):
                    dma("pool", WUF[g].t(), dr(wuf_d[g], "wuf"), "wuf%d" % g)
            f1_body(pump0=norm_steps([2, 3], h_out(PV_G1)) + as1, pump1=as2 + norm_steps([0, 1], h_out(PV_GM)),
                    after_gu=wuf_prefetch)
            tap("x1", X.t(), [128, 8, S], F32)
            run_all(norm_steps([2, 3], h_out(PV_GM)))
            mixer(after_merge=f2_pre if last >= 4 else None)
        else:
            f1_body(pump0=norm_steps([2, 3], h_out(PV_G1)))
        if last >= 4:
            tap("x2", X.t(), [128, 8, S], F32)
            f2_body(pump1=norm_steps([0, 1], f_out))
            run_all(norm_steps([2, 3], f_out))
        else:
            run_all(norm_steps(range(NT), f_out))

        sc.finalize()
        esem = {k: es.enter_context(nc.semaphore("s_" + k)) for k in ("pe", "act", "dve", "pool", "sp")}
        dsem = {k: es.enter_context(nc.semaphore("d_" + k)) for k in sc.dcnt}
        final_waits = [(dsem[k], v) for k, v in sc.dcnt.items()]
        with nc.Block() as block:
            @block.tensor
            def _(e):
                sc.emit("pe", e, esem, dsem)

            @block.scalar
            def _(e):
                sc.emit("act", e, esem, dsem)

            @block.vector
            def _(e):
                sc.emit("dve", e, esem, dsem)

            @block.gpsimd
            def _(e):
                sc.emit("pool", e, esem, dsem)

            @block.sync
            def _(e):
                sc.emit("sp", e, esem, dsem)
                for sem, v in final_waits:
                    e.wait_ge(sem, v)
    return nc, sc


def _tile_w(w, ncols_chunk):
    K, N = w.shape
    return np.ascontiguousarray(w.reshape(K // 128, 128, N // ncols_chunk, ncols_chunk).transpose(2, 1, 0, 3))


def _dft_consts():
    c = np.arange(128)
    ang = 2.0 * np.pi * np.outer(c, c) / 128.0
    cc = np.concatenate([np.cos(ang), np.sin(ang)], axis=1).astype(ml_dtypes.bfloat16)
    s = np.arange(S, dtype=np.int64)
    prod = np.outer(s, s) % S
    ang = 2.0 * np.pi * prod.astype(np.float64) / S
    sc_ = 1.0 / 512.0
    cs = (np.cos(ang) * sc_).astype(np.float32)
    sn = (-np.sin(ang) * sc_).astype(np.float32)
    out = np.empty((4, 2, 128, 16 * 512), dtype=ml_dtypes.bfloat16)
    for t, M in enumerate((cs, sn)):
        M4 = M.reshape(16, 128, 4, 512).transpose(2, 1, 0, 3)
        out[:, t] = M4.reshape(4, 128, 16 * 512).astype(ml_dtypes.bfloat16)
    return cc, out


_CONSTS = None


def prepare_inputs(inp):
    global _CONSTS
    if _CONSTS is None:
        _CONSTS = _dft_consts()
    cc, slabs = _CONSTS
    f = lambda k: np.asarray(inp[k], dtype=np.float32)
    pos = np.asarray(inp["positions"])
    if not np.array_equal(pos, np.broadcast_to(np.arange(S, dtype=pos.dtype), pos.shape)):
        raise NotImplementedError("kernel supports positions == arange(SEQ) (as produced by setup_inputs)")
    shared = {}
    for i, p in ((1, "ffn1"), (2, "ffn2")):
        wg = _tile_w(f(p + "_wg")[0], 128)
        wu = _tile_w(f(p + "_wu")[0], 128)
        shared["wgu%d" % i] = np.ascontiguousarray(np.stack([wg, wu], axis=2)).reshape(NF, 128, 2 * 8 * 128)
        shared["wd%d" % i] = _tile_w(f(p + "_wd")[0], 128).reshape(8, 128, NF * 128)
    w_in = f("w_in")[0]
    shared["wuf"] = _tile_w(w_in[:, 0:512], 128).reshape(4, 128, 8 * 128)
    shared["wv"] = _tile_w(w_in[:, 1536:2048], 512).reshape(128, 8 * 512)
    wq = _tile_w(w_in[:, 512:1024], 128)
    wk = _tile_w(w_in[:, 1024:1536], 128)
    shared["wqk"] = np.ascontiguousarray(np.stack([wq, wk], axis=2)).reshape(4, 128, 2 * 8 * 128)
    wga = _tile_w(w_in[:, 2048:3072], 128)
    wgb = _tile_w(w_in[:, 3072:4096], 128)
    shared["wgate"] = np.ascontiguousarray(np.stack([wga, wgb], axis=2)).reshape(8, 128, 2 * 8 * 128)
    wfo = _tile_w(f("w_fourier_out")[0], 128)
    wao = _tile_w(f("w_attn_out")[0], 128)
    shared["wfa"] = np.ascontiguousarray(np.stack([wfo, wao], axis=2)).reshape(8, 128, 2 * 4 * 128)
    shared["wout"] = _tile_w(f("w_out")[0], 128).reshape(8, 128, 8 * 128)
    pvec = np.zeros((128, PV_N), np.float32)
    for col, k in ((PV_G1, "ffn1_norm"), (PV_GM, "mix_norm"), (PV_G2, "ffn2_norm"), (PV_GF, "final_norm")):
        pvec[:, col:col + 8] = f(k).reshape(8, 128).T
    pvec[:, PV_SUB] = f("subln_g").reshape(128)
    rb = f("rel_bias")
    pvec[:, PV_RBLO:PV_RBLO + 4] = rb[15][None, :]
    pvec[:, PV_RBHI:PV_RBHI + 4] = rb[31][None, :]
    for j, k in enumerate(("lambda_q1", "lambda_k1", "lambda_q2", "lambda_k2")):
        pvec[:, PV_LAM + 64 * j:PV_LAM + 64 * j + 64] = f(k).reshape(1, 64)
    shared["pvec"] = pvec
    shared["rbrep"] = np.ascontiguousarray(np.repeat(rb[:, :, None], 128, axis=2)).reshape(32, 4 * 128)
    shared["cc"] = cc
    shared["slabs"] = slabs
    x = f("x")
    in_maps = []
    for b in range(x.shape[0]):
        m = dict(shared)
        m["xT"] = np.ascontiguousarray(x[b].T)
        in_maps.append(m)
    return in_maps


_NC = None


def kernel(**inputs):
    global _NC
    in_maps = prepare_inputs(inputs)
    if _NC is None:
        _NC = build_program()[0]
    res = run_bass_kernel_spmd(_NC, in_maps, core_ids=list(range(len(in_maps))))
    out = np.stack([np.ascontiguousarray(r["outT"].T) for r in res.results], axis=0)
    return out.astype(np.float32)
```

```python
import bisect
import math
from contextlib import ExitStack

import numpy as np
import ml_dtypes

import concourse.bass as bass
import concourse.mybir as mybir
from concourse.bass_utils import run_bass_kernel_spmd

F32 = mybir.dt.float32
BF16 = mybir.dt.bfloat16
AF = mybir.ActivationFunctionType
ALU = mybir.AluOpType
AX = mybir.AxisListType

S = 2048
D = 1024
DFF = 2816
NF = 22
NT = 4
NORM_EPS = 1e-6
SUBLN_EPS = 1e-5
LAMBDA_INIT = 0.8 - 0.6 * math.exp(-0.3 * 0)
T5_THRESH = [1, 2, 3, 4, 5, 6, 7, 8, 12, 16, 23, 32, 46, 64, 91]

PV_G1, PV_GM, PV_G2, PV_GF, PV_SUB, PV_RBLO, PV_RBHI, PV_LAM, PV_N = 0, 8, 16, 24, 32, 33, 37, 48, 304


class TL:
    __slots__ = ("ap", "space", "s", "e")

    def __init__(self, ap, space, s, e):
        self.ap, self.space, self.s, self.e = ap, space, s, e


class Buf:
    def __init__(self, space, ap2d, off, esz, A, B):
        self.space, self.off, self.esz, self.A, self.B = space, off, esz, A, B
        self.ap3 = ap2d.rearrange("p (a b) -> p a b", a=A)

    def t(self, a=None, b=None, p=None):
        A, B = self.A, self.B
        if a is None:
            a0, a1, ai = 0, A, slice(0, A)
        elif isinstance(a, tuple):
            a0, a1 = a
            ai = slice(a0, a1)
        else:
            a0, a1, ai = a, a + 1, a
        b0, b1 = (0, B) if b is None else b
        ps = slice(0, 128) if p is None else slice(p[0], p[1])
        ap = self.ap3[ps, ai, b0:b1]
        s = self.off + (a0 * B + b0) * self.esz
        e = self.off + ((a1 - 1) * B + b1) * self.esz
        return TL(ap, self.space, s, e)


class Op:
    __slots__ = ("eng", "fn", "raw", "oth", "idx", "signal", "dkey", "dcum", "tick")


class Sched:
    def __init__(self):
        self.ops = []
        self.rec = {}
        self.dcnt = {}

    def _overlaps(self, space, s, e):
        if space not in self.rec:
            self.rec[space] = []
        recs = self.rec[space]
        i = bisect.bisect_right([r[0] for r in recs], s) - 1 if len(recs) > 64 else 0
        if i < 0:
            i = 0
        out = []
        for j in range(i, len(recs)):
            r = recs[j]
            if r[0] >= e:
                break
            if r[1] > s:
                out.append(j)
        return recs, out

    def add(self, eng, fn, reads=(), writes=(), dkey=None):
        op = Op()
        op.eng, op.fn, op.idx, op.signal, op.dkey, op.tick = eng, fn, len(self.ops), False, dkey, None
        if dkey is not None:
            self.dcnt[dkey] = self.dcnt.get(dkey, 0) + 16
            op.dcum = self.dcnt[dkey]
            tok = ("d", dkey, op.dcum)
        else:
            op.dcum = None
            tok = ("e", eng, op.idx)
        raw, oth = {}, {}

        def note(d, t):
            k = (t[0], t[1])
            if d.get(k, -1) < t[2]:
                d[k] = t[2]

        for tl in reads:
            recs, idxs = self._overlaps(tl.space, tl.s, tl.e)
            for j in idxs:
                r = recs[j]
                if r[2] is not None:
                    note(raw, r[2])
                note(r[3], tok)
        for tl in writes:
            recs, idxs = self._overlaps(tl.space, tl.s, tl.e)
            new = []
            for j in idxs:
                r = recs[j]
                if r[2] is not None:
                    note(oth, r[2])
                for k, v in r[3].items():
                    note(oth, (k[0], k[1], v))
                if r[0] < tl.s:
                    new.append([r[0], tl.s, r[2], dict(r[3])])
                if r[1] > tl.e:
                    new.append([tl.e, r[1], r[2], dict(r[3])])
            for j in reversed(idxs):
                del recs[j]
            new.append([tl.s, tl.e, tok, {}])
            recs.extend(new)
            recs.sort(key=lambda r: r[0])
        me = (tok[0], tok[1])
        for d in (raw, oth):
            if me in d and d[me] == tok[2]:
                del d[me]
        op.raw, op.oth = raw, oth
        self.ops.append(op)
        return op

    def finalize(self):
        for op in self.ops:
            keep = {}
            for d, is_raw in ((op.raw, True), (op.oth, False)):
                for (kind, name), val in d.items():
                    if kind == "e" and name == op.eng and op.eng == "pe":
                        continue
                    k = (kind, name)
                    if keep.get(k, -1) < val:
                        keep[k] = val
            op.raw = keep
            for (kind, name), val in keep.items():
                if kind == "e":
                    self.ops[val].signal = True
        cnt = {}
        for op in self.ops:
            if op.dkey is None and op.signal:
                cnt[op.eng] = cnt.get(op.eng, 0) + 1
                op.tick = cnt[op.eng]
        self.nsig = cnt

    def emit(self, eng, e, esem, dsem):
        waited = {}
        n = 0
        for op in self.ops:
            if op.eng != eng:
                continue
            for (kind, name), val in op.raw.items():
                if kind == "e":
                    sem, v = esem[name], self.ops[val].tick
                else:
                    sem, v = dsem[name], val
                k = (kind, name)
                if waited.get(k, -1) >= v:
                    continue
                waited[k] = v
                e.wait_ge(sem, v)
            ins = op.fn(e)
            if op.dkey is not None:
                ins.then_inc(dsem[op.dkey], 16)
            elif op.signal:
                ins.then_inc(esem[eng], 1)
            n += 1
        return n


def build_program(stop_after=None, taps=()):
    nc = bass.Bass("TRN2", target_bir_lowering=False)
    sc = Sched()
    taps = set(taps)

    def dram(name, shape, dt, kind="ExternalInput"):
        return nc.dram_tensor(name, shape, dt, kind=kind)

    xT_d = dram("xT", [D, S], F32).ap()
    wgu_d = [dram("wgu%d" % i, [NF, 128, 2 * 8 * 128], F32).ap() for i in (1, 2)]
    wd_d = [dram("wd%d" % i, [8, 128, NF * 128], F32).ap() for i in (1, 2)]
    wuf_d = dram("wuf", [4, 128, 8 * 128], F32).ap()
    wv_d = dram("wv", [128, 8 * 512], F32).ap()
    wqk_d = dram("wqk", [4, 128, 2 * 8 * 128], F32).ap()
    wgate_d = dram("wgate", [8, 128, 2 * 8 * 128], F32).ap()
    wfa_d = dram("wfa", [8, 128, 2 * 4 * 128], F32).ap()
    wout_d = dram("wout", [8, 128, 8 * 128], F32).ap()
    pvec_d = dram("pvec", [128, PV_N], F32).ap()
    rbrep_d = dram("rbrep", [32, 4 * 128], F32).ap()
    cc_d = dram("cc", [128, 256], BF16).ap()
    slabs_d = dram("slabs", [4, 2, 128, 16 * 512], BF16).ap()
    outT_d = dram("outT", [D, S], F32, kind="ExternalOutput").ap()
    tscr_h = dram("tscr", [4, 128 * 512], BF16, kind="Internal")
    tscr_d = tscr_h.ap()
    tap_d = {}

    es = ExitStack()
    with es:
        Xt = es.enter_context(nc.sbuf_tensor("X", [128, 8 * S], F32))
        Ht = es.enter_context(nc.sbuf_tensor("H", [128, 8 * S], BF16))
        Ct = es.enter_context(nc.sbuf_tensor("CST", [128, 8192], BF16))
        SCR_B = 94208
        St = es.enter_context(nc.sbuf_tensor("SCR", [128, SCR_B // 2], BF16))
        PS = [es.enter_context(nc.psum_tensor("ps%d" % i, [128, 512], F32)) for i in range(8)]
        psT = [TL(PS[i][:], "ps%d" % i, 0, 1) for i in range(8)]

        def pst(i, c0=0, c1=512, p=None):
            ap = PS[i][:, c0:c1] if p is None else PS[i][p[0]:p[1], c0:c1]
            return TL(ap, "ps%d" % i, 0, 1)

        X_OFF, H_OFF, C_OFF, S_OFF = 0, 1 << 20, 2 << 20, 3 << 20
        X = Buf("sb", Xt[:], X_OFF, 4, 8, S)
        H = Buf("sb", Ht[:], H_OFF, 2, 8, S)

        def cbuf(off, nbytes, dt, A, B):
            esz = 4 if dt == F32 else 2
            assert off % 4 == 0 and off + nbytes <= 16384 and A * B * esz == nbytes, (off, nbytes, A, B)
            ap = Ct[:, off // 2:(off + nbytes) // 2]
            if dt == F32:
                ap = ap.bitcast(F32)
            return Buf("sb", ap, C_OFF + off, esz, A, B)

        def sbuf(off, dt, A, B):
            esz = 4 if dt == F32 else 2
            nbytes = A * B * esz
            assert off % 4 == 0 and off + nbytes <= SCR_B, (off, nbytes)
            ap = St[:, off // 2:(off + nbytes) // 2]
            if dt == F32:
                ap = ap.bitcast(F32)
            return Buf("sb", ap, S_OFF + off, esz, A, B)

        PV = cbuf(0, PV_N * 4, F32, 1, PV_N)
        LAMT = cbuf(1216, 64 * 4, F32, 1, 64)
        LAMS = cbuf(1472, 8 * 4, F32, 1, 8)
        ONES1 = cbuf(1504, 256, BF16, 1, 128)
        ONESM = cbuf(1760, 256, BF16, 1, 128)
        ONESV = cbuf(2016, 256, BF16, 1, 128)
        IDENT = cbuf(2272, 256, BF16, 1, 128)
        CC = cbuf(2528, 512, BF16, 1, 256)
        BW = cbuf(3040, 4 * 384 * 2, BF16, 4, 384)
        SQ = cbuf(6112, 4 * 1024, BF16, 4, 512)
        RSTD = cbuf(11232, 2 * 2048, F32, 2, 512)
        PIDX = cbuf(15328, 4, F32, 1, 1)
        assert 15332 <= 16384

        def pv(col, n=1):
            return PV.t(0, (col, col + n))

        def mm(ps, lhsT, rhs, start, stop):
            rd = [lhsT, rhs] + ([] if start else [ps])
            sc.add("pe", lambda e: e.matmul(ps.ap, lhsT.ap, rhs.ap, start=start, stop=stop),
                   reads=rd, writes=[ps])

        def act(out, in_, func, bias=None, scale=1.0):
            rd = [in_] + ([bias] if isinstance(bias, TL) else [])
            b = bias.ap if isinstance(bias, TL) else (0.0 if bias is None else bias)
            sc.add("act", lambda e: e.activation(out=out.ap, in_=in_.ap, func=func, bias=b, scale=scale),
                   reads=rd, writes=[out])

        def vop(eng, name, out, ins, *args, **kw):
            def fn(e):
                k = dict(kw)
                for key, tl in ins:
                    k[key] = tl.ap
                return getattr(e, name)(out=out.ap, **k)
            sc.add(eng, fn, reads=[tl for _, tl in ins], writes=[out])

        def tt(eng, out, a, b, op):
            vop(eng, "tensor_tensor", out, [("in0", a), ("in1", b)], op=op)

        def stt(eng, out, in0, scalar, in1, op0, op1):
            if isinstance(scalar, TL):
                vop(eng, "scalar_tensor_tensor", out, [("in0", in0), ("scalar", scalar), ("in1", in1)], op0=op0, op1=op1)
            else:
                vop(eng, "scalar_tensor_tensor", out, [("in0", in0), ("in1", in1)], scalar=scalar, op0=op0, op1=op1)

        def ts(eng, out, in0, s1, s2, op0, op1=None):
            ins = [("in0", in0)]
            kw = {}
            if isinstance(s1, TL):
                ins.append(("scalar1", s1))
            else:
                kw["scalar1"] = s1
            if isinstance(s2, TL):
                ins.append(("scalar2", s2))
            else:
                kw["scalar2"] = s2
            kw["op0"] = op0
            if op1 is not None:
                kw["op1"] = op1
            vop(eng, "tensor_scalar", out, ins, **kw)

        def tss(eng, out, in_, scalar, op):
            if isinstance(scalar, TL):
                vop(eng, "tensor_single_scalar", out, [("in_", in_), ("scalar", scalar)], op=op)
            else:
                vop(eng, "tensor_single_scalar", out, [("in_", in_)], scalar=scalar, op=op)

        def rstd_from(rs, ps, eps):
            act(rs, ps, AF.Sqrt, bias=eps)
            vop("dve", "reciprocal", rs, [("in_", rs)])

        def cp(eng, out, in_):
            vop(eng, "tensor_copy", out, [("in_", in_)])

        def memset(eng, out, val):
            sc.add(eng, lambda e: e.memset(out.ap, val), writes=[out])

        def dma(eng, out, in_, key):
            sc.add(eng, lambda e: e.dma_start(out=out.ap, in_=in_.ap), reads=[in_], writes=[out], dkey=key)

        def dr(ap, name):
            return TL(ap, "dram:" + name, 0, 1)

        def tap(name, buf_tile, shape, dt):
            if name not in taps:
                return
            t = dram("tap_" + name, shape, dt, kind="ExternalOutput").ap()
            tap_d[name] = t
            dma("sp", TL(t, "dram:tap_" + name, 0, 1), buf_tile, "tap_" + name)

        xv = xT_d.rearrange("(c p) t -> p c t", p=128)
        dma("sp", PV.t(0), dr(pvec_d, "pvec"), "pv")
        for tb in range(NT):
            for c in range(8):
                dma("sp", X.t(c, (tb * 512, tb * 512 + 512)), dr(xv[:, c, tb * 512:tb * 512 + 512], "xT"), "x%d_%d" % (tb, c))
        dma("sp", CC.t(0), dr(cc_d, "cc"), "cc")
        memset("dve", ONES1.t(0), 1.0)
        memset("dve", ONESM.t(0), 1.0 / 1024.0)
        memset("dve", ONESV.t(0), 1.0 / 128.0)

        def norm_steps(tbs, emit_out, fast=False):
            steps = []
            for tb in tbs:
                tsl = (tb * 512, tb * 512 + 512)

                def sqf(c, tsl=tsl):
                    if c < 8:
                        eng = ("pool", "act", "dve")[c % 3] if fast else "pool"
                        if eng == "act":
                            act(SQ.t(c % 4), X.t(c, tsl), AF.Square)
                        else:
                            tt(eng, SQ.t(c % 4), X.t(c, tsl), X.t(c, tsl), ALU.mult)

                def mmf(c):
                    mm(psT[6], ONESM.t(0), SQ.t(c % 4), c == 0, c == 7)

                steps.append(lambda sqf=sqf: (sqf(0), sqf(1), sqf(2)))
                steps.append(lambda sqf=sqf: sqf(3))
                for c in range(8):
                    steps.append(lambda c=c, sqf=sqf, mmf=mmf: (mmf(c), sqf(c + 4)))
                steps.append(lambda tb=tb: rstd_from(RSTD.t(tb % 2), psT[6], NORM_EPS))
                for c0 in (0, 4):
                    steps.append(lambda tb=tb, c0=c0: [emit_out(tb, c, RSTD.t(tb % 2)) for c in range(c0, c0 + 4)])
            return steps

        def h_out(gcol):
            def f(tb, c, rs):
                tsl = (tb * 512, tb * 512 + 512)
                stt("dve", H.t(c, tsl), X.t(c, tsl), pv(gcol + c), rs, ALU.mult, ALU.mult)
            return f

        def run_all(steps):
            for st in steps:
                st()

        def make_ffn(wgu, wd, wname):
            A = sbuf(0, BF16, NF, 1024)
            WD = [sbuf(45056 + i * 5632, BF16, NF, 128) for i in range(3)]
            SG = [sbuf(61952 + i * 2048, F32, 1, 512) for i in range(2)]
            WGU = [sbuf(81920 + i * 4096, BF16, 16, 128) for i in range(3)]
            seq_gu = [(th, f) for th in range(2) for f in range(NF)]
            seq_d = [(th, dc) for th in range(2) for dc in range(8)]

            def load_gu(i):
                th, f = seq_gu[i]
                dma("pool", WGU[i % 3].t(), dr(wgu[f], wname + "gu"), "%sgu%d" % (wname, i % 3))

            def load_d(i):
                th, dc = seq_d[i]
                dma("pool", WD[i % 3].t(), dr(wd[dc], wname + "d"), "%sd%d" % (wname, i % 3))

            def prefetch():
                load_gu(0)
                load_gu(1)

            def body(pump0=(), pump1=(), hooks=None, after_gu=None):
                pumps = [list(pump0), list(pump1)]
                hooks = hooks or {}
                ev = 0
                for i, (th, f) in enumerate(seq_gu):
                    pump = pumps[th]
                    if i + 2 < len(seq_gu):
                        load_gu(i + 2)
                    if f == NF - 4:
                        load_d(th * 8)
                    if f == NF - 2:
                        load_d(th * 8 + 1)
                    w = WGU[i % 3]
                    if (th, f) in hooks:
                        hooks[(th, f)]()
                    for tb2 in range(2):
                        tb = th * 2 + tb2
                        tsl = (tb * 512, tb * 512 + 512)
                        pg, pu = (0, 1) if ev % 2 == 0 else (2, 3)
                        for k in range(8):
                            mm(psT[pg], w.t(k), H.t(k, tsl), k == 0, k == 7)
                        for k in range(8):
                            mm(psT[pu], w.t(8 + k), H.t(k, tsl), k == 0, k == 7)
                        sg = SG[ev % 2].t(0)
                        act(sg, psT[pg], AF.Silu)
                        tt("dve", A.t(f, (tb2 * 512, tb2 * 512 + 512)), sg, psT[pu], ALU.mult)
                        ev += 1
                        if pump:
                            pump.pop(0)()
                    if f == NF - 1:
                        if th == 1 and after_gu is not None:
                            after_gu()
                        for dc in range(8):
                            j = th * 8 + dc
                            if dc + 2 < 8:
                                load_d(j + 2)
                            wdt = WD[j % 3]
                            for tb2 in range(2):
                                tb = th * 2 + tb2
                                tsl = (tb * 512, tb * 512 + 512)
                                py = 4 + (dc * 2 + tb2) % 2
                                for ff in range(NF):
                                    mm(psT[py], wdt.t(ff), A.t(ff, (tb2 * 512, tb2 * 512 + 512)), ff == 0, ff == NF - 1)
                                stt("dve", X.t(dc, tsl), psT[py], 0.5, X.t(dc, tsl), ALU.mult, ALU.add)
                                if pump:
                                    pump.pop(0)()
                        run_all(pump)
                        del pump[:]
            return prefetch, body

        OUT = [sbuf(66048 + i * 2048, F32, 1, 512) for i in range(4)]
        ov = outT_d.rearrange("(c p) t -> p c t", p=128)
        out_n = [0]

        def f_out(tb, c, rs):
            tsl = (tb * 512, tb * 512 + 512)
            o = OUT[out_n[0] % 4].t(0)
            stt("dve", o, X.t(c, tsl), pv(PV_GF + c), rs, ALU.mult, ALU.mult)
            dma("sp", TL(ov[:, c, tsl[0]:tsl[1]], "dram:outT%d_%d" % (c, tb), 0, 1), o, "out%d" % (out_n[0] % 4))
            out_n[0] += 1

        def attn_setup_steps():
            TT = sbuf(36864, BF16, 4, 512)
            RV = sbuf(66048, F32, 1, 512)
            NV = sbuf(68096, F32, 1, 512)
            CNT = sbuf(70144, F32, 1, 512)
            RBF = sbuf(72192, F32, 1, 512)
            OH = sbuf(74240, BF16, 1, 512)
            RBH = sbuf(75264, BF16, 1, 512)
            IDF = sbuf(77312, F32, 1, 128)
            P32 = (0, 32)
            rv, nv, cnt = RV.t(0, p=P32), NV.t(0, p=P32), CNT.t(0, p=P32)
            p1 = []

            def lam():
                for j in range(2):
                    tt("dve", LAMT.t(0), pv(PV_LAM + 128 * j, 64), pv(PV_LAM + 128 * j + 64, 64), ALU.mult)
                    vop("dve", "tensor_reduce", LAMS.t(0, (j, j + 1)), [("in_", LAMT.t(0))], axis=AX.X, op=ALU.add)
            p1.append(lam)

            def lam2():
                act(LAMS.t(0, (2, 4)), LAMS.t(0, (0, 2)), AF.Exp)
                ts("dve", LAMS.t(0, (4, 5)), LAMS.t(0, (2, 3)), LAMS.t(0, (3, 4)), LAMBDA_INIT, ALU.subtract, ALU.add)
                tss("dve", LAMS.t(0, (5, 6)), LAMS.t(0, (4, 5)), -1.0, ALU.mult)
                tss("dve", LAMS.t(0, (6, 7)), pv(PV_SUB), 1.0 - LAMBDA_INIT, ALU.mult)
            p1.append(lam2)

            def idx():
                sc.add("pool", lambda e: e.iota(PIDX.t(0).ap, [[0, 1]], base=0, channel_multiplier=1,
                                                allow_small_or_imprecise_dtypes=True), writes=[PIDX.t(0)])
                sc.add("pool", lambda e: e.iota(IDF.t(0).ap, [[1, 128]], base=0, channel_multiplier=0,
                                                allow_small_or_imprecise_dtypes=True), writes=[IDF.t(0)])
                sc.add("pool", lambda e: e.iota(rv.ap, [[-1, 512]], base=256, channel_multiplier=0,
                                                allow_small_or_imprecise_dtypes=True), writes=[rv])
                dma("sp", RBF.t(0, p=P32), dr(rbrep_d, "rbrep"), "rbrep")
            p1.append(idx)

            def idn():
                tss("dve", IDENT.t(0), IDF.t(0), PIDX.t(0), ALU.is_equal)
                tt("dve", nv, rv, rv, ALU.mult)
                tss("dve", cnt, nv, float(T5_THRESH[0] ** 2), ALU.is_ge)
            p1.append(idn)
            for i in range(1, len(T5_THRESH), 2):
                def thr(i=i):
                    for t in T5_THRESH[i:i + 2]:
                        stt("dve", cnt, nv, float(t * t), cnt, ALU.is_ge, ALU.add)
                p1.append(thr)

            def fin():
                ts("dve", rv, rv, 0.0, 16.0, ALU.is_gt, ALU.mult)
                tt("dve", cnt, cnt, rv, ALU.add)
                tss("dve", OH.t(0, p=P32), cnt, PIDX.t(0, p=P32), ALU.is_equal)
                cp("dve", RBH.t(0, p=P32), RBF.t(0, p=P32))
            p1.append(fin)
            p2 = []
            for h in range(4):
                def tabh(h=h):
                    mm(psT[7], RBH.t(0, (h * 128, h * 128 + 128), p=P32), OH.t(0, p=P32), True, True)
                    cp("dve", TT.t(h), psT[7])
                p2.append(tabh)

            def toep():
                dma("sp", TL(tscr_d.rearrange("j (p m) -> p j m", p=128), "dram:tscr", 0, 1), TT.t(), "tscr_w")
                win = bass.AP(tscr_h, 128, [[511, 128], [65536, 4], [1, 384]])
                dma("sp", BW.t(), TL(win, "dram:tscr", 0, 1), "tscr_r")
            p2.append(toep)
            return p1, p2

        evac_rr = [0]

        def evac(out, ps, scale=None):
            evac_rr[0] += 1
            if scale is not None:
                act(out, ps, AF.Copy, scale=scale)
            elif evac_rr[0] % 2 == 0:
                act(out, ps, AF.Copy)
            else:
                cp("dve", out, ps)

        def mixer(after_merge=None):
            FB = sbuf(0, BF16, 4, S)
            ACAS = sbuf(16384, BF16, 16, 1024)
            SLAB = [sbuf(49152 + i * 16384, BF16, 16, 512) for i in range(2)]
            WUF = [sbuf(81920 + i * 2048, BF16, 8, 128) for i in range(2)]
            bank = [0]

            def nb(n=4):
                bank[0] = (bank[0] + 1) % n
                return bank[0]

            for g in range(4):
                if g >= 2:
                    dma("pool", WUF[g % 2].t(), dr(wuf_d[g], "wuf"), "wuf%d" % (g % 2))
                for tb in range(NT):
                    tsl = (tb * 512, tb * 512 + 512)
                    b = nb()
                    for k in range(8):
                        mm(psT[b], WUF[g % 2].t(k), H.t(k, tsl), k == 0, k == 7)
                    evac(FB.t(g, tsl), psT[b])
            tap("uf", FB.t(), [128, 4, S], BF16)
            for s_c in range(16):
                ssl = (s_c * 128, s_c * 128 + 128)
                for gp in range(2):
                    b = nb()
                    for gi in range(2):
                        g = gp * 2 + gi
                        mm(pst(b, gi * 256, gi * 256 + 256), FB.t(g, ssl), CC.t(0), True, True)
                    evac(ACAS.t(s_c, (gp * 512, gp * 512 + 512)), psT[b])
            def load_slab(i):
                sb_, trig = divmod(i, 2)
                dma("sp", SLAB[trig].t(), dr(slabs_d[sb_, trig], "slabs"), "slab%d" % trig)
            load_slab(0)
            load_slab(1)
            for sb_ in range(4):
                for trig in range(2):
                    for g in range(4):
                        for s_c in range(16):
                            mm(psT[g], ACAS.t(s_c, (g * 256 + trig * 128, g * 256 + trig * 128 + 128)),
                               SLAB[trig].t(s_c), trig == 0 and s_c == 0, trig == 1 and s_c == 15)
                    if sb_ < 3:
                        load_slab((sb_ + 1) * 2 + trig)
                for g in range(4):
                    evac(FB.t(g, (sb_ * 512, sb_ * 512 + 512)), psT[g])
            tap("yf", FB.t(), [128, 4, S], BF16)
            if stop_after == "m1":
                return

            VALL = sbuf(16384, BF16, 16, 512)
            OB = sbuf(32768, BF16, 4, S)
            QB = [sbuf(49152 + i * 4096, BF16, 1, S) for i in range(2)]
            KP = [[sbuf(57344 + i * 8192 + j * 4096, BF16, 1, S) for j in range(2)] for i in range(2)]
            WQK = [sbuf(73728 + i * 4096, BF16, 16, 128) for i in range(2)]
            WV = sbuf(81920, BF16, 8, 512)
            PT = [sbuf(81920 + i * 1024, BF16, 1, 512) for i in range(4)]
            OT = [sbuf(86016 + i * 2048, F32, 1, 512) for i in range(4)]
            dma("pool", WV.t(), dr(wv_d, "wv"), "wv")
            dma("pool", WQK[0].t(), dr(wqk_d[0], "wqk"), "wqk0")
            for i in range(2):
                memset("dve", KP[i][0].t(0, p=(64, 128)), 0.0)
                memset("dve", KP[i][1].t(0, p=(0, 64)), 0.0)
            for s_c in range(16):
                b = nb()
                for k in range(8):
                    mm(psT[b], H.t(k, (s_c * 128, s_c * 128 + 128)), WV.t(k), k == 0, k == 7)
                evac(VALL.t(s_c), psT[b])
            deferred = []

            def flush_deferred(bank):
                while deferred:
                    deferred.pop(0)(bank)

            WG0 = sbuf(73728, BF16, 16, 128)
            WFA0 = sbuf(57344, BF16, 8, 128)
            for h in range(4):
                hs = h % 2
                if h + 1 < 4:
                    dma("pool", WQK[(h + 1) % 2].t(), dr(wqk_d[h + 1], "wqk"), "wqk%d" % ((h + 1) % 2))
                elif stop_after != "m2":
                    dma("pool", WG0.t(), dr(wgate_d[0], "wgate"), "wg_first")
                    dma("pool", WFA0.t(), dr(wfa_d[0], "wfa"), "wfa_first")
                for tb in range(NT):
                    tsl = (tb * 512, tb * 512 + 512)
                    b = nb()
                    for k in range(8):
                        mm(psT[b], WQK[hs].t(k), H.t(k, tsl), k == 0, k == 7)
                    evac(QB[hs].t(0, tsl), psT[b], scale=0.125)
                    b = nb()
                    for k in range(8):
                        mm(psT[b], WQK[hs].t(8 + k), H.t(k, tsl), k == 0, k == 7)
                    cp("dve", KP[hs][0].t(0, tsl, p=(0, 64)), pst(b, p=(0, 64)))
                    act(KP[hs][1].t(0, tsl, p=(64, 128)), pst(b, p=(64, 128)), AF.Copy)
                def scores(qb, kc):
                    q0 = qb * 512
                    pair = (0, 1) if kc % 2 == 0 else (2, 3)
                    ksl = (kc * 128, kc * 128 + 128)
                    b0 = max(128 * (kc - 1), q0)
                    b1 = min(128 * (kc + 2), q0 + 512)
                    for m in range(2):
                        has_band = b1 > b0
                        mm(psT[pair[m]], KP[hs][m].t(0, ksl), QB[hs].t(0, (q0, q0 + 512)), True, not has_band)
                        if has_band:
                            mm(pst(pair[m], b0 - q0, b1 - q0), IDENT.t(0),
                               BW.t(h, (b0 - 128 * (kc - 1), b1 - 128 * (kc - 1))), False, True)
                    return pair, b0, b1

                def exps(qb, kc, pair, b0, b1):
                    q0 = qb * 512
                    for m in range(2):
                        P = PT[(kc % 2) * 2 + m]
                        segs = []
                        if b1 > b0:
                            if b0 > q0:
                                segs.append((q0, b0, pv(PV_RBHI + h)))
                            segs.append((b0, b1, None))
                            if b1 < q0 + 512:
                                segs.append((b1, q0 + 512, pv(PV_RBLO + h)))
                        elif 128 * (kc - 1) >= q0 + 512:
                            segs.append((q0, q0 + 512, pv(PV_RBHI + h)))
                        else:
                            segs.append((q0, q0 + 512, pv(PV_RBLO + h)))
                        for (c0, c1, bias) in segs:
                            act(P.t(0, (c0 - q0, c1 - q0)), pst(pair[m], c0 - q0, c1 - q0), AF.Exp, bias=bias)

                def av(kc):
                    for m in range(2):
                        P = PT[(kc % 2) * 2 + m].t(0)
                        mm(psT[4 + m], VALL.t(kc, (h * 128, h * 128 + 128)), P, kc == 0, kc == 15)
                        mm(psT[6 + m], ONES1.t(0), P, kc == 0, kc == 15)

                def end_of_block(qb):
                    qsl = (qb * 512, qb * 512 + 512)
                    act(OT[1].t(0), psT[4], AF.Copy)
                    vop("dve", "reciprocal", OT[0].t(0), [("in_", psT[6])])
                    act(OT[3].t(0), psT[5], AF.Copy)
                    vop("dve", "reciprocal", OT[2].t(0), [("in_", psT[7])])
                    tt("dve", OT[1].t(0), OT[1].t(0), OT[0].t(0), ALU.mult)
                    stt("dve", OT[3].t(0), OT[3].t(0), LAMS.t(0, (5, 6)), OT[2].t(0), ALU.mult, ALU.mult)
                    tt("pool", OT[1].t(0), OT[1].t(0), OT[3].t(0), ALU.add)
                    tt("pool", SQ.t(0), OT[1].t(0), OT[1].t(0), ALU.mult)

                    def subln(bank, h=h, qsl=qsl):
                        mm(psT[bank], ONESV.t(0), SQ.t(0), True, True)
                        act(RSTD.t(0), psT[bank], AF.Ln, bias=SUBLN_EPS)
                        act(RSTD.t(0), RSTD.t(0), AF.Exp, scale=-0.5)
                        stt("dve", OB.t(h, qsl), OT[1].t(0), LAMS.t(0, (6, 7)), RSTD.t(0), ALU.mult, ALU.mult)
                    deferred.append(subln)

                seq = [(qb, kc) for qb in range(NT) for kc in range(16)]
                info = scores(*seq[0])
                for i, (qb, kc) in enumerate(seq):
                    nxt = scores(*seq[i + 1]) if i + 1 < len(seq) else None
                    exps(qb, kc, *info)
                    av(kc)
                    info = nxt
                    if kc == 7:
                        flush_deferred(2)
                    if kc == 15:
                        end_of_block(qb)
            if stop_after == "m2":
                flush_deferred(2)
            tap("o", OB.t(), [128, 4, S], BF16)
            if stop_after == "m2":
                return

            MRG = sbuf(49152, BF16, 8, S)
            WG = [sbuf(16384 + i * 8192, BF16, 16, 128) for i in range(2)]
            WFA = [sbuf(16384 + 4096 + i * 8192, BF16, 8, 128) for i in range(2)]
            SGA = [sbuf(81920 + i * 2048, F32, 1, 512) for i in range(2)]
            SGB = [sbuf(86016 + i * 4096, F32, 1, 512) for i in range(2)]

            def load_m(dc):
                if dc == 0:
                    return
                dma("pool", WG[dc % 2].t(), dr(wgate_d[dc], "wgate"), "wg%d" % (dc % 2))
                dma("pool", WFA[dc % 2].t(), dr(wfa_d[dc], "wfa"), "wfa%d" % (dc % 2))
            WOR = [sbuf(16384 + dc * 2048, BF16, 8, 128) for dc in range(8)]
            load_m(0)
            it = 0
            for dc in range(8):
                if dc + 1 < 8:
                    load_m(dc + 1)
                else:
                    for d2 in range(4):
                        dma("pool", WOR[d2].t(), dr(wout_d[d2], "wout"), "wo%d" % d2)
                for tb in range(NT):
                    tsl = (tb * 512, tb * 512 + 512)
                    bs = (0, 1, 2, 3) if it % 2 == 0 else (4, 5, 6, 7)
                    wg_ = WG0 if dc == 0 else WG[dc % 2]
                    wfa_ = WFA0 if dc == 0 else WFA[dc % 2]
                    for k in range(8):
                        mm(psT[bs[2]], wg_.t(k), H.t(k, tsl), k == 0, k == 7)
                    for k in range(8):
                        mm(psT[bs[3]], wg_.t(8 + k), H.t(k, tsl), k == 0, k == 7)
                    for k in range(4):
                        mm(psT[bs[0]], wfa_.t(k), FB.t(k, tsl), k == 0, k == 3)
                    for k in range(4):
                        mm(psT[bs[1]], wfa_.t(4 + k), OB.t(k, tsl), k == 0, k == 3)
                    act(SGA[it % 2].t(0), psT[bs[2]], AF.Sigmoid)
                    act(SGB[it % 2].t(0), psT[bs[3]], AF.Sigmoid)
                    tt("dve", SGA[it % 2].t(0), SGA[it % 2].t(0), psT[bs[0]], ALU.mult)
                    tt("dve", SGB[it % 2].t(0), SGB[it % 2].t(0), psT[bs[1]], ALU.mult)
                    tt("pool", MRG.t(dc, tsl), SGA[it % 2].t(0), SGB[it % 2].t(0), ALU.add)
                    it += 1
                    if it == 2:
                        flush_deferred(0)
            tap("mrg", MRG.t(), [128, 8, S], BF16)
            for dc in range(4, 8):
                dma("pool", WOR[dc].t(), dr(wout_d[dc], "wout"), "wo%d" % dc)
            if after_merge is not None:
                after_merge()
            pend = []
            for tb in range(NT):
                tsl = (tb * 512, tb * 512 + 512)
                for dc in range(8):
                    for _ in range(2):
                        if pend:
                            pend.pop(0)()
                    b = nb()
                    for k in range(8):
                        mm(psT[b], WOR[dc].t(k), MRG.t(k, tsl), k == 0, k == 7)
                    tt("dve", X.t(dc, tsl), psT[b], X.t(dc, tsl), ALU.add)
                pend.extend(norm_steps([tb], h_out(PV_G2)))
            run_all(pend)

        STAGES = ["ffn1", "m1", "m2", "m3", "ffn2"]
        last = STAGES.index(stop_after) if stop_after else len(STAGES) - 1
        f1_pre, f1_body = make_ffn(wgu_d[0], wd_d[0], "f1")
        f2_pre, f2_body = make_ffn(wgu_d[1], wd_d[1], "f2")
        f1_pre()
        run_all(norm_steps([0, 1], h_out(PV_G1), fast=True))
        if last >= 1:
            as1, as2 = attn_setup_steps()

            def wuf_prefetch():
                WUF = [sbuf(81920 + i * 2048, BF16, 8, 128) for i in range(2)]
                for g in range(2):
                    dma("pool", WUF[g].t(), dr(wuf_d[g], "wuf"), "wuf%d" % g)
            f1_body(pump0=norm_steps([2, 3], h_out(PV_G1)) + as1, pump1=as2 + norm_steps([0, 1], h_out(PV_GM)),
                    after_gu=wuf_prefetch)
            tap("x1", X.t(), [128, 8, S], F32)
            run_all(norm_steps([2, 3], h_out(PV_GM), fast=True))
            mixer(after_merge=f2_pre if last >= 4 else None)
        else:
            f1_body(pump0=norm_steps([2, 3], h_out(PV_G1)))
        if last >= 4:
            tap("x2", X.t(), [128, 8, S], F32)
            f2_body(pump1=norm_steps([0, 1], f_out))
            run_all(norm_steps([2, 3], f_out, fast=True))
        else:
            run_all(norm_steps(range(NT), f_out))

        sc.finalize()
        esem = {k: es.enter_context(nc.semaphore("s_" + k)) for k in ("pe", "act", "dve", "pool", "sp")}
        dsem = {k: es.enter_context(nc.semaphore("d_" + k)) for k in sc.dcnt}
        final_waits = [(dsem[k], v) for k, v in sc.dcnt.items()]
        with nc.Block() as block:
            @block.tensor
            def _(e):
                sc.emit("pe", e, esem, dsem)

            @block.scalar
            def _(e):
                sc.emit("act", e, esem, dsem)

            @block.vector
            def _(e):
                sc.emit("dve", e, esem, dsem)

            @block.gpsimd
            def _(e):
                sc.emit("pool", e, esem, dsem)

            @block.sync
            def _(e):
                sc.emit("sp", e, esem, dsem)
                for sem, v in final_waits:
                    e.wait_ge(sem, v)
    return nc, sc


def _tile_w(w, ncols_chunk):
    K, N = w.shape
    return np.ascontiguousarray(w.reshape(K // 128, 128, N // ncols_chunk, ncols_chunk).transpose(2, 1, 0, 3))


def _dft_consts():
    c = np.arange(128)
    ang = 2.0 * np.pi * np.outer(c, c) / 128.0
    cc = np.concatenate([np.cos(ang), np.sin(ang)], axis=1).astype(ml_dtypes.bfloat16)
    s = np.arange(S, dtype=np.int64)
    prod = np.outer(s, s) % S
    ang = 2.0 * np.pi * prod.astype(np.float64) / S
    sc_ = 1.0 / 512.0
    cs = (np.cos(ang) * sc_).astype(np.float32)
    sn = (-np.sin(ang) * sc_).astype(np.float32)
    out = np.empty((4, 2, 128, 16 * 512), dtype=ml_dtypes.bfloat16)
    for t, M in enumerate((cs, sn)):
        M4 = M.reshape(16, 128, 4, 512).transpose(2, 1, 0, 3)
        out[:, t] = M4.reshape(4, 128, 16 * 512).astype(ml_dtypes.bfloat16)
    return cc, out


_CONSTS = None


def prepare_inputs(inp):
    global _CONSTS
    if _CONSTS is None:
        _CONSTS = _dft_consts()
    cc, slabs = _CONSTS
    f = lambda k: np.asarray(inp[k], dtype=np.float32)
    pos = np.asarray(inp["positions"])
    if not np.array_equal(pos, np.broadcast_to(np.arange(S, dtype=pos.dtype), pos.shape)):
        raise NotImplementedError("kernel supports positions == arange(SEQ) (as produced by setup_inputs)")
    shared = {}
    for i, p in ((1, "ffn1"), (2, "ffn2")):
        wg = _tile_w(f(p + "_wg")[0], 128)
        wu = _tile_w(f(p + "_wu")[0], 128)
        shared["wgu%d" % i] = np.ascontiguousarray(np.stack([wg, wu], axis=2)).reshape(NF, 128, 2 * 8 * 128)
        shared["wd%d" % i] = _tile_w(f(p + "_wd")[0], 128).reshape(8, 128, NF * 128)
    w_in = f("w_in")[0]
    shared["wuf"] = _tile_w(w_in[:, 0:512], 128).reshape(4, 128, 8 * 128)
    shared["wv"] = _tile_w(w_in[:, 1536:2048], 512).reshape(128, 8 * 512)
    wq = _tile_w(w_in[:, 512:1024], 128)
    wk = _tile_w(w_in[:, 1024:1536], 128)
    shared["wqk"] = np.ascontiguousarray(np.stack([wq, wk], axis=2)).reshape(4, 128, 2 * 8 * 128)
    wga = _tile_w(w_in[:, 2048:3072], 128)
    wgb = _tile_w(w_in[:, 3072:4096], 128)
    shared["wgate"] = np.ascontiguousarray(np.stack([wga, wgb], axis=2)).reshape(8, 128, 2 * 8 * 128)
    wfo = _tile_w(f("w_fourier_out")[0], 128)
    wao = _tile_w(f("w_attn_out")[0], 128)
    shared["wfa"] = np.ascontiguousarray(np.stack([wfo, wao], axis=2)).reshape(8, 128, 2 * 4 * 128)
    shared["wout"] = _tile_w(f("w_out")[0], 128).reshape(8, 128, 8 * 128)
    pvec = np.zeros((128, PV_N), np.float32)
    for col, k in ((PV_G1, "ffn1_norm"), (PV_GM, "mix_norm"), (PV_G2, "ffn2_norm"), (PV_GF, "final_norm")):
        pvec[:, col:col + 8] = f(k).reshape(8, 128).T
    pvec[:, PV_SUB] = f("subln_g").reshape(128)
    rb = f("rel_bias")
    pvec[:, PV_RBLO:PV_RBLO + 4] = rb[15][None, :]
    pvec[:, PV_RBHI:PV_RBHI + 4] = rb[31][None, :]
    for j, k in enumerate(("lambda_q1", "lambda_k1", "lambda_q2", "lambda_k2")):
        pvec[:, PV_LAM + 64 * j:PV_LAM + 64 * j + 64] = f(k).reshape(1, 64)
    shared["pvec"] = pvec
    shared["rbrep"] = np.ascontiguousarray(np.repeat(rb[:, :, None], 128, axis=2)).reshape(32, 4 * 128)
    shared["cc"] = cc
    shared["slabs"] = slabs
    x = f("x")
    in_maps = []
    for b in range(x.shape[0]):
        m = dict(shared)
        m["xT"] = np.ascontiguousarray(x[b].T)
        in_maps.append(m)
    return in_maps


_NC = None


def kernel(**inputs):
    global _NC
    in_maps = prepare_inputs(inputs)
    if _NC is None:
        _NC = build_program()[0]
    res = run_bass_kernel_spmd(_NC, in_maps, core_ids=list(range(len(in_maps))))
    out = np.stack([np.ascontiguousarray(r["outT"].T) for r in res.results], axis=0)
    return out.astype(np.float32)
```

```python
import bisect
import math
from contextlib import ExitStack

import numpy as np
import ml_dtypes

import concourse.bass as bass
import concourse.mybir as mybir
from concourse.bass_utils import run_bass_kernel_spmd

F32 = mybir.dt.float32
BF16 = mybir.dt.bfloat16
AF = mybir.ActivationFunctionType
ALU = mybir.AluOpType
AX = mybir.AxisListType

S = 2048
D = 1024
DFF = 2816
NF = 22
NT = 4
NORM_EPS = 1e-6
SUBLN_EPS = 1e-5
LAMBDA_INIT = 0.8 - 0.6 * math.exp(-0.3 * 0)
T5_THRESH = [1, 2, 3, 4, 5, 6, 7, 8, 12, 16, 23, 32, 46, 64, 91]

PV_G1, PV_GM, PV_G2, PV_GF, PV_SUB, PV_RBLO, PV_RBHI, PV_LAM, PV_N = 0, 8, 16, 24, 32, 33, 37, 48, 304


class TL:
    __slots__ = ("ap", "space", "s", "e")

    def __init__(self, ap, space, s, e):
        self.ap, self.space, self.s, self.e = ap, space, s, e


class Buf:
    def __init__(self, space, ap2d, off, esz, A, B):
        self.space, self.off, self.esz, self.A, self.B = space, off, esz, A, B
        self.ap3 = ap2d.rearrange("p (a b) -> p a b", a=A)

    def t(self, a=None, b=None, p=None):
        A, B = self.A, self.B
        if a is None:
            a0, a1, ai = 0, A, slice(0, A)
        elif isinstance(a, tuple):
            a0, a1 = a
            ai = slice(a0, a1)
        else:
            a0, a1, ai = a, a + 1, a
        b0, b1 = (0, B) if b is None else b
        ps = slice(0, 128) if p is None else slice(p[0], p[1])
        ap = self.ap3[ps, ai, b0:b1]
        s = self.off + (a0 * B + b0) * self.esz
        e = self.off + ((a1 - 1) * B + b1) * self.esz
        return TL(ap, self.space, s, e)


class Op:
    __slots__ = ("eng", "fn", "raw", "oth", "idx", "signal", "dkey", "dcum", "tick")


class Sched:
    def __init__(self):
        self.ops = []
        self.rec = {}
        self.dcnt = {}

    def _overlaps(self, space, s, e):
        if space not in self.rec:
            self.rec[space] = []
        recs = self.rec[space]
        i = bisect.bisect_right([r[0] for r in recs], s) - 1 if len(recs) > 64 else 0
        if i < 0:
            i = 0
        out = []
        for j in range(i, len(recs)):
            r = recs[j]
            if r[0] >= e:
                break
            if r[1] > s:
                out.append(j)
        return recs, out

    def add(self, eng, fn, reads=(), writes=(), dkey=None):
        op = Op()
        op.eng, op.fn, op.idx, op.signal, op.dkey, op.tick = eng, fn, len(self.ops), False, dkey, None
        if dkey is not None:
            self.dcnt[dkey] = self.dcnt.get(dkey, 0) + 16
            op.dcum = self.dcnt[dkey]
            tok = ("d", dkey, op.dcum)
        else:
            op.dcum = None
            tok = ("e", eng, op.idx)
        raw, oth = {}, {}

        def note(d, t):
            k = (t[0], t[1])
            if d.get(k, -1) < t[2]:
                d[k] = t[2]

        for tl in reads:
            recs, idxs = self._overlaps(tl.space, tl.s, tl.e)
            for j in idxs:
                r = recs[j]
                if r[2] is not None:
                    note(raw, r[2])
                note(r[3], tok)
        for tl in writes:
            recs, idxs = self._overlaps(tl.space, tl.s, tl.e)
            new = []
            for j in idxs:
                r = recs[j]
                if r[2] is not None:
                    note(oth, r[2])
                for k, v in r[3].items():
                    note(oth, (k[0], k[1], v))
                if r[0] < tl.s:
                    new.append([r[0], tl.s, r[2], dict(r[3])])
                if r[1] > tl.e:
                    new.append([tl.e, r[1], r[2], dict(r[3])])
            for j in reversed(idxs):
                del recs[j]
            new.append([tl.s, tl.e, tok, {}])
            recs.extend(new)
            recs.sort(key=lambda r: r[0])
        me = (tok[0], tok[1])
        for d in (raw, oth):
            if me in d and d[me] == tok[2]:
                del d[me]
        op.raw, op.oth = raw, oth
        self.ops.append(op)
        return op

    def finalize(self):
        for op in self.ops:
            keep = {}
            for d, is_raw in ((op.raw, True), (op.oth, False)):
                for (kind, name), val in d.items():
                    if kind == "e" and name == op.eng and op.eng == "pe":
                        continue
                    k = (kind, name)
                    if keep.get(k, -1) < val:
                        keep[k] = val
            op.raw = keep
            for (kind, name), val in keep.items():
                if kind == "e":
                    self.ops[val].signal = True
        cnt = {}
        for op in self.ops:
            if op.dkey is None and op.signal:
                cnt[op.eng] = cnt.get(op.eng, 0) + 1
                op.tick = cnt[op.eng]
        self.nsig = cnt

    def emit(self, eng, e, esem, dsem):
        waited = {}
        n = 0
        for op in self.ops:
            if op.eng != eng:
                continue
            for (kind, name), val in op.raw.items():
                if kind == "e":
                    sem, v = esem[name], self.ops[val].tick
                else:
                    sem, v = dsem[name], val
                k = (kind, name)
                if waited.get(k, -1) >= v:
                    continue
                waited[k] = v
                e.wait_ge(sem, v)
            ins = op.fn(e)
            if op.dkey is not None:
                ins.then_inc(dsem[op.dkey], 16)
            elif op.signal:
                ins.then_inc(esem[eng], 1)
            n += 1
        return n


def build_program(stop_after=None, taps=()):
    nc = bass.Bass("TRN2", target_bir_lowering=False)
    sc = Sched()
    taps = set(taps)

    def dram(name, shape, dt, kind="ExternalInput"):
        return nc.dram_tensor(name, shape, dt, kind=kind)

    xT_d = dram("xT", [D, S], F32).ap()
    wgu_d = [dram("wgu%d" % i, [NF, 128, 2 * 8 * 128], F32).ap() for i in (1, 2)]
    wd_d = [dram("wd%d" % i, [8, 128, NF * 128], F32).ap() for i in (1, 2)]
    wuf_d = dram("wuf", [4, 128, 8 * 128], F32).ap()
    wv_d = dram("wv", [128, 8 * 512], F32).ap()
    wqk_d = dram("wqk", [4, 128, 2 * 8 * 128], F32).ap()
    wgate_d = dram("wgate", [8, 128, 2 * 8 * 128], F32).ap()
    wfa_d = dram("wfa", [8, 128, 2 * 4 * 128], F32).ap()
    wout_d = dram("wout", [8, 128, 8 * 128], F32).ap()
    pvec_d = dram("pvec", [128, PV_N], F32).ap()
    rbrep_d = dram("rbrep", [32, 4 * 128], F32).ap()
    cc_d = dram("cc", [128, 256], BF16).ap()
    slabs_d = dram("slabs", [4, 2, 128, 16 * 512], BF16).ap()
    outT_d = dram("outT", [D, S], F32, kind="ExternalOutput").ap()
    tscr_h = dram("tscr", [4, 128 * 512], BF16, kind="Internal")
    tscr_d = tscr_h.ap()
    tap_d = {}

    es = ExitStack()
    with es:
        Xt = es.enter_context(nc.sbuf_tensor("X", [128, 8 * S], F32))
        Ht = es.enter_context(nc.sbuf_tensor("H", [128, 8 * S], BF16))
        Ct = es.enter_context(nc.sbuf_tensor("CST", [128, 8192], BF16))
        SCR_B = 94208
        St = es.enter_context(nc.sbuf_tensor("SCR", [128, SCR_B // 2], BF16))
        PS = [es.enter_context(nc.psum_tensor("ps%d" % i, [128, 512], F32)) for i in range(8)]
        psT = [TL(PS[i][:], "ps%d" % i, 0, 1) for i in range(8)]

        def pst(i, c0=0, c1=512, p=None):
            ap = PS[i][:, c0:c1] if p is None else PS[i][p[0]:p[1], c0:c1]
            return TL(ap, "ps%d" % i, 0, 1)

        X_OFF, H_OFF, C_OFF, S_OFF = 0, 1 << 20, 2 << 20, 3 << 20
        X = Buf("sb", Xt[:], X_OFF, 4, 8, S)
        H = Buf("sb", Ht[:], H_OFF, 2, 8, S)

        def cbuf(off, nbytes, dt, A, B):
            esz = 4 if dt == F32 else 2
            assert off % 4 == 0 and off + nbytes <= 16384 and A * B * esz == nbytes, (off, nbytes, A, B)
            ap = Ct[:, off // 2:(off + nbytes) // 2]
            if dt == F32:
                ap = ap.bitcast(F32)
            return Buf("sb", ap, C_OFF + off, esz, A, B)

        def sbuf(off, dt, A, B):
            esz = 4 if dt == F32 else 2
            nbytes = A * B * esz
            assert off % 4 == 0 and off + nbytes <= SCR_B, (off, nbytes)
            ap = St[:, off // 2:(off + nbytes) // 2]
            if dt == F32:
                ap = ap.bitcast(F32)
            return Buf("sb", ap, S_OFF + off, esz, A, B)

        PV = cbuf(0, PV_N * 4, F32, 1, PV_N)
        LAMT = cbuf(1216, 64 * 4, F32, 1, 64)
        LAMS = cbuf(1472, 8 * 4, F32, 1, 8)
        ONES1 = cbuf(1504, 256, BF16, 1, 128)
        ONESM = cbuf(1760, 256, BF16, 1, 128)
        ONESV = cbuf(2016, 256, BF16, 1, 128)
        IDENT = cbuf(2272, 256, BF16, 1, 128)
        CC = cbuf(2528, 512, BF16, 1, 256)
        BW = cbuf(3040, 4 * 384 * 2, BF16, 4, 384)
        SQ = cbuf(6112, 4 * 1024, BF16, 4, 512)
        RSTD = cbuf(11232, 2 * 2048, F32, 2, 512)
        PIDX = cbuf(15328, 4, F32, 1, 1)
        assert 15332 <= 16384

        def pv(col, n=1):
            return PV.t(0, (col, col + n))

        def mm(ps, lhsT, rhs, start, stop):
            rd = [lhsT, rhs] + ([] if start else [ps])
            sc.add("pe", lambda e: e.matmul(ps.ap, lhsT.ap, rhs.ap, start=start, stop=stop),
                   reads=rd, writes=[ps])

        def act(out, in_, func, bias=None, scale=1.0):
            rd = [in_] + ([bias] if isinstance(bias, TL) else [])
            b = bias.ap if isinstance(bias, TL) else (0.0 if bias is None else bias)
            sc.add("act", lambda e: e.activation(out=out.ap, in_=in_.ap, func=func, bias=b, scale=scale),
                   reads=rd, writes=[out])

        def vop(eng, name, out, ins, *args, **kw):
            def fn(e):
                k = dict(kw)
                for key, tl in ins:
                    k[key] = tl.ap
                return getattr(e, name)(out=out.ap, **k)
            sc.add(eng, fn, reads=[tl for _, tl in ins], writes=[out])

        def tt(eng, out, a, b, op):
            vop(eng, "tensor_tensor", out, [("in0", a), ("in1", b)], op=op)

        def stt(eng, out, in0, scalar, in1, op0, op1):
            if isinstance(scalar, TL):
                vop(eng, "scalar_tensor_tensor", out, [("in0", in0), ("scalar", scalar), ("in1", in1)], op0=op0, op1=op1)
            else:
                vop(eng, "scalar_tensor_tensor", out, [("in0", in0), ("in1", in1)], scalar=scalar, op0=op0, op1=op1)

        def ts(eng, out, in0, s1, s2, op0, op1=None):
            ins = [("in0", in0)]
            kw = {}
            if isinstance(s1, TL):
                ins.append(("scalar1", s1))
            else:
                kw["scalar1"] = s1
            if isinstance(s2, TL):
                ins.append(("scalar2", s2))
            else:
                kw["scalar2"] = s2
            kw["op0"] = op0
            if op1 is not None:
                kw["op1"] = op1
            vop(eng, "tensor_scalar", out, ins, **kw)

        def tss(eng, out, in_, scalar, op):
            if isinstance(scalar, TL):
                vop(eng, "tensor_single_scalar", out, [("in_", in_), ("scalar", scalar)], op=op)
            else:
                vop(eng, "tensor_single_scalar", out, [("in_", in_)], scalar=scalar, op=op)

        def rstd_from(rs, ps, eps):
            act(rs, ps, AF.Ln, bias=eps)
            act(rs, rs, AF.Exp, scale=-0.5)

        def cp(eng, out, in_):
            vop(eng, "tensor_copy", out, [("in_", in_)])

        def memset(eng, out, val):
            sc.add(eng, lambda e: e.memset(out.ap, val), writes=[out])

        def dma(eng, out, in_, key):
            sc.add(eng, lambda e: e.dma_start(out=out.ap, in_=in_.ap), reads=[in_], writes=[out], dkey=key)

        def dr(ap, name):
            return TL(ap, "dram:" + name, 0, 1)

        def tap(name, buf_tile, shape, dt):
            if name not in taps:
                return
            t = dram("tap_" + name, shape, dt, kind="ExternalOutput").ap()
            tap_d[name] = t
            dma("sp", TL(t, "dram:tap_" + name, 0, 1), buf_tile, "tap_" + name)

        xv = xT_d.rearrange("(c p) t -> p c t", p=128)
        dma("sp", PV.t(0), dr(pvec_d, "pvec"), "pv")
        for tb in range(NT):
            for c in range(8):
                dma("sp", X.t(c, (tb * 512, tb * 512 + 512)), dr(xv[:, c, tb * 512:tb * 512 + 512], "xT"), "x%d_%d" % (tb, c))
        dma("sp", CC.t(0), dr(cc_d, "cc"), "cc")
        memset("dve", ONES1.t(0), 1.0)
        memset("dve", ONESM.t(0), 1.0 / 1024.0)
        memset("dve", ONESV.t(0), 1.0 / 128.0)

        def norm_steps(tbs, emit_out, fast=False):
            steps = []
            for tb in tbs:
                tsl = (tb * 512, tb * 512 + 512)

                def sqf(c, tsl=tsl):
                    if c < 8:
                        eng = ("pool", "act", "dve")[c % 3] if fast else "pool"
                        if eng == "act":
                            act(SQ.t(c % 4), X.t(c, tsl), AF.Square)
                        else:
                            tt(eng, SQ.t(c % 4), X.t(c, tsl), X.t(c, tsl), ALU.mult)

                def mmf(c):
                    mm(psT[6], ONESM.t(0), SQ.t(c % 4), c == 0, c == 7)

                steps.append(lambda sqf=sqf: (sqf(0), sqf(1), sqf(2)))
                steps.append(lambda sqf=sqf: sqf(3))
                for c in range(8):
                    steps.append(lambda c=c, sqf=sqf, mmf=mmf: (mmf(c), sqf(c + 4)))
                steps.append(lambda tb=tb: rstd_from(RSTD.t(tb % 2), psT[6], NORM_EPS))
                for c0 in (0, 4):
                    steps.append(lambda tb=tb, c0=c0: [emit_out(tb, c, RSTD.t(tb % 2)) for c in range(c0, c0 + 4)])
            return steps

        def h_out(gcol):
            def f(tb, c, rs):
                tsl = (tb * 512, tb * 512 + 512)
                stt("dve", H.t(c, tsl), X.t(c, tsl), pv(gcol + c), rs, ALU.mult, ALU.mult)
            return f

        def run_all(steps):
            for st in steps:
                st()

        def make_ffn(wgu, wd, wname):
            A = sbuf(0, BF16, NF, 1024)
            WD = [sbuf(45056 + i * 5632, BF16, NF, 128) for i in range(3)]
            SG = [sbuf(61952 + i * 2048, F32, 1, 512) for i in range(2)]
            WGU = [sbuf(81920 + i * 4096, BF16, 16, 128) for i in range(3)]
            seq_gu = [(th, f) for th in range(2) for f in range(NF)]
            seq_d = [(th, dc) for th in range(2) for dc in range(8)]

            def load_gu(i):
                th, f = seq_gu[i]
                dma("pool", WGU[i % 3].t(), dr(wgu[f], wname + "gu"), "%sgu%d" % (wname, i % 3))

            def load_d(i):
                th, dc = seq_d[i]
                dma("pool", WD[i % 3].t(), dr(wd[dc], wname + "d"), "%sd%d" % (wname, i % 3))

            def prefetch():
                load_gu(0)
                load_gu(1)

            def body(pump0=(), pump1=(), hooks=None, after_gu=None):
                pumps = [list(pump0), list(pump1)]
                hooks = hooks or {}
                ev = 0
                for i, (th, f) in enumerate(seq_gu):
                    pump = pumps[th]
                    if i + 2 < len(seq_gu):
                        load_gu(i + 2)
                    if f == NF - 4:
                        load_d(th * 8)
                    if f == NF - 2:
                        load_d(th * 8 + 1)
                    w = WGU[i % 3]
                    if (th, f) in hooks:
                        hooks[(th, f)]()
                    for tb2 in range(2):
                        tb = th * 2 + tb2
                        tsl = (tb * 512, tb * 512 + 512)
                        pg, pu = (0, 1) if ev % 2 == 0 else (2, 3)
                        for k in range(8):
                            mm(psT[pg], w.t(k), H.t(k, tsl), k == 0, k == 7)
                        for k in range(8):
                            mm(psT[pu], w.t(8 + k), H.t(k, tsl), k == 0, k == 7)
                        sg = SG[ev % 2].t(0)
                        act(sg, psT[pg], AF.Silu)
                        tt("dve", A.t(f, (tb2 * 512, tb2 * 512 + 512)), sg, psT[pu], ALU.mult)
                        ev += 1
                        if pump:
                            pump.pop(0)()
                    if f == NF - 1:
                        if th == 1 and after_gu is not None:
                            after_gu()
                        for dc in range(8):
                            j = th * 8 + dc
                            if dc + 2 < 8:
                                load_d(j + 2)
                            wdt = WD[j % 3]
                            for tb2 in range(2):
                                tb = th * 2 + tb2
                                tsl = (tb * 512, tb * 512 + 512)
                                py = 4 + (dc * 2 + tb2) % 2
                                for ff in range(NF):
                                    mm(psT[py], wdt.t(ff), A.t(ff, (tb2 * 512, tb2 * 512 + 512)), ff == 0, ff == NF - 1)
                                stt("dve", X.t(dc, tsl), psT[py], 0.5, X.t(dc, tsl), ALU.mult, ALU.add)
                                if pump:
                                    pump.pop(0)()
                        run_all(pump)
                        del pump[:]
            return prefetch, body

        OUT = [sbuf(66048 + i * 2048, F32, 1, 512) for i in range(4)]
        ov = outT_d.rearrange("(c p) t -> p c t", p=128)
        out_n = [0]

        def f_out(tb, c, rs):
            tsl = (tb * 512, tb * 512 + 512)
            o = OUT[out_n[0] % 4].t(0)
            stt("dve", o, X.t(c, tsl), pv(PV_GF + c), rs, ALU.mult, ALU.mult)
            dma("sp", TL(ov[:, c, tsl[0]:tsl[1]], "dram:outT%d_%d" % (c, tb), 0, 1), o, "out%d" % (out_n[0] % 4))
            out_n[0] += 1

        def attn_setup_steps():
            TT = sbuf(36864, BF16, 4, 512)
            RV = sbuf(66048, F32, 1, 512)
            NV = sbuf(68096, F32, 1, 512)
            CNT = sbuf(70144, F32, 1, 512)
            RBF = sbuf(72192, F32, 1, 512)
            OH = sbuf(74240, BF16, 1, 512)
            RBH = sbuf(75264, BF16, 1, 512)
            IDF = sbuf(77312, F32, 1, 128)
            P32 = (0, 32)
            rv, nv, cnt = RV.t(0, p=P32), NV.t(0, p=P32), CNT.t(0, p=P32)
            p1 = []

            def lam():
                for j in range(2):
                    tt("dve", LAMT.t(0), pv(PV_LAM + 128 * j, 64), pv(PV_LAM + 128 * j + 64, 64), ALU.mult)
                    vop("dve", "tensor_reduce", LAMS.t(0, (j, j + 1)), [("in_", LAMT.t(0))], axis=AX.X, op=ALU.add)
            p1.append(lam)

            def lam2():
                act(LAMS.t(0, (2, 4)), LAMS.t(0, (0, 2)), AF.Exp)
                ts("dve", LAMS.t(0, (4, 5)), LAMS.t(0, (2, 3)), LAMS.t(0, (3, 4)), LAMBDA_INIT, ALU.subtract, ALU.add)
                tss("dve", LAMS.t(0, (5, 6)), LAMS.t(0, (4, 5)), -1.0, ALU.mult)
                tss("dve", LAMS.t(0, (6, 7)), pv(PV_SUB), 1.0 - LAMBDA_INIT, ALU.mult)
            p1.append(lam2)

            def idx():
                sc.add("pool", lambda e: e.iota(PIDX.t(0).ap, [[0, 1]], base=0, channel_multiplier=1,
                                                allow_small_or_imprecise_dtypes=True), writes=[PIDX.t(0)])
                sc.add("pool", lambda e: e.iota(IDF.t(0).ap, [[1, 128]], base=0, channel_multiplier=0,
                                                allow_small_or_imprecise_dtypes=True), writes=[IDF.t(0)])
                sc.add("pool", lambda e: e.iota(rv.ap, [[-1, 512]], base=256, channel_multiplier=0,
                                                allow_small_or_imprecise_dtypes=True), writes=[rv])
                dma("sp", RBF.t(0, p=P32), dr(rbrep_d, "rbrep"), "rbrep")
            p1.append(idx)

            def idn():
                tss("dve", IDENT.t(0), IDF.t(0), PIDX.t(0), ALU.is_equal)
                tt("dve", nv, rv, rv, ALU.mult)
                tss("dve", cnt, nv, float(T5_THRESH[0] ** 2), ALU.is_ge)
            p1.append(idn)
            for i in range(1, len(T5_THRESH), 2):
                def thr(i=i):
                    for t in T5_THRESH[i:i + 2]:
                        stt("dve", cnt, nv, float(t * t), cnt, ALU.is_ge, ALU.add)
                p1.append(thr)

            def fin():
                ts("dve", rv, rv, 0.0, 16.0, ALU.is_gt, ALU.mult)
                tt("dve", cnt, cnt, rv, ALU.add)
                tss("dve", OH.t(0, p=P32), cnt, PIDX.t(0, p=P32), ALU.is_equal)
                cp("dve", RBH.t(0, p=P32), RBF.t(0, p=P32))
            p1.append(fin)
            p2 = []
            for h in range(4):
                def tabh(h=h):
                    mm(psT[7], RBH.t(0, (h * 128, h * 128 + 128), p=P32), OH.t(0, p=P32), True, True)
                    cp("dve", TT.t(h), psT[7])
                p2.append(tabh)

            def toep():
                dma("sp", TL(tscr_d.rearrange("j (p m) -> p j m", p=128), "dram:tscr", 0, 1), TT.t(), "tscr_w")
                win = bass.AP(tscr_h, 128, [[511, 128], [65536, 4], [1, 384]])
                dma("sp", BW.t(), TL(win, "dram:tscr", 0, 1), "tscr_r")
            p2.append(toep)
            return p1, p2

        evac_rr = [0]

        def evac(out, ps, scale=None):
            evac_rr[0] += 1
            if scale is not None:
                act(out, ps, AF.Copy, scale=scale)
            elif evac_rr[0] % 2 == 0:
                act(out, ps, AF.Copy)
            else:
                cp("dve", out, ps)

        def mixer(after_merge=None):
            FB = sbuf(0, BF16, 4, S)
            ACAS = sbuf(16384, BF16, 16, 1024)
            SLAB = [sbuf(49152 + i * 16384, BF16, 16, 512) for i in range(2)]
            WUF = [sbuf(81920 + i * 2048, BF16, 8, 128) for i in range(2)]
            bank = [0]

            def nb(n=4):
                bank[0] = (bank[0] + 1) % n
                return bank[0]

            for g in range(4):
                if g >= 2:
                    dma("pool", WUF[g % 2].t(), dr(wuf_d[g], "wuf"), "wuf%d" % (g % 2))
                for tb in range(NT):
                    tsl = (tb * 512, tb * 512 + 512)
                    b = nb()
                    for k in range(8):
                        mm(psT[b], WUF[g % 2].t(k), H.t(k, tsl), k == 0, k == 7)
                    evac(FB.t(g, tsl), psT[b])
            tap("uf", FB.t(), [128, 4, S], BF16)
            for s_c in range(16):
                ssl = (s_c * 128, s_c * 128 + 128)
                for gp in range(2):
                    b = nb()
                    for gi in range(2):
                        g = gp * 2 + gi
                        mm(pst(b, gi * 256, gi * 256 + 256), FB.t(g, ssl), CC.t(0), True, True)
                    evac(ACAS.t(s_c, (gp * 512, gp * 512 + 512)), psT[b])
            def load_slab(i):
                sb_, trig = divmod(i, 2)
                dma("sp", SLAB[trig].t(), dr(slabs_d[sb_, trig], "slabs"), "slab%d" % trig)
            load_slab(0)
            load_slab(1)
            for sb_ in range(4):
                for trig in range(2):
                    for g in range(4):
                        for s_c in range(16):
                            mm(psT[g], ACAS.t(s_c, (g * 256 + trig * 128, g * 256 + trig * 128 + 128)),
                               SLAB[trig].t(s_c), trig == 0 and s_c == 0, trig == 1 and s_c == 15)
                    if sb_ < 3:
                        load_slab((sb_ + 1) * 2 + trig)
                for g in range(4):
                    evac(FB.t(g, (sb_ * 512, sb_ * 512 + 512)), psT[g])
            tap("yf", FB.t(), [128, 4, S], BF16)
            if stop_after == "m1":
                return

            VALL = sbuf(16384, BF16, 16, 512)
            OB = sbuf(32768, BF16, 4, S)
            QB = [sbuf(49152 + i * 4096, BF16, 1, S) for i in range(2)]
            KP = [[sbuf(57344 + i * 8192 + j * 4096, BF16, 1, S) for j in range(2)] for i in range(2)]
            WQK = [sbuf(73728 + i * 4096, BF16, 16, 128) for i in range(2)]
            WV = sbuf(81920, BF16, 8, 512)
            PT = [sbuf(81920 + i * 1024, BF16, 1, 512) for i in range(4)]
            OT = [sbuf(86016 + i * 2048, F32, 1, 512) for i in range(4)]
            dma("pool", WV.t(), dr(wv_d, "wv"), "wv")
            dma("pool", WQK[0].t(), dr(wqk_d[0], "wqk"), "wqk0")
            for i in range(2):
                memset("dve", KP[i][0].t(0, p=(64, 128)), 0.0)
                memset("dve", KP[i][1].t(0, p=(0, 64)), 0.0)
            for s_c in range(16):
                b = nb()
                for k in range(8):
                    mm(psT[b], H.t(k, (s_c * 128, s_c * 128 + 128)), WV.t(k), k == 0, k == 7)
                evac(VALL.t(s_c), psT[b])
            deferred = []

            def flush_deferred(bank):
                while deferred:
                    deferred.pop(0)(bank)

            WG0 = sbuf(73728, BF16, 16, 128)
            WFA0 = sbuf(57344, BF16, 8, 128)
            for h in range(4):
                hs = h % 2
                if h + 1 < 4:
                    dma("pool", WQK[(h + 1) % 2].t(), dr(wqk_d[h + 1], "wqk"), "wqk%d" % ((h + 1) % 2))
                elif stop_after != "m2":
                    dma("pool", WG0.t(), dr(wgate_d[0], "wgate"), "wg_first")
                    dma("pool", WFA0.t(), dr(wfa_d[0], "wfa"), "wfa_first")
                for tb in range(NT):
                    tsl = (tb * 512, tb * 512 + 512)
                    b = nb()
                    for k in range(8):
                        mm(psT[b], WQK[hs].t(k), H.t(k, tsl), k == 0, k == 7)
                    evac(QB[hs].t(0, tsl), psT[b], scale=0.125)
                    b = nb()
                    for k in range(8):
                        mm(psT[b], WQK[hs].t(8 + k), H.t(k, tsl), k == 0, k == 7)
                    cp("dve", KP[hs][0].t(0, tsl, p=(0, 64)), pst(b, p=(0, 64)))
                    act(KP[hs][1].t(0, tsl, p=(64, 128)), pst(b, p=(64, 128)), AF.Copy)
                def scores(qb, kc):
                    q0 = qb * 512
                    pair = (0, 1) if kc % 2 == 0 else (2, 3)
                    ksl = (kc * 128, kc * 128 + 128)
                    b0 = max(128 * (kc - 1), q0)
                    b1 = min(128 * (kc + 2), q0 + 512)
                    for m in range(2):
                        has_band = b1 > b0
                        mm(psT[pair[m]], KP[hs][m].t(0, ksl), QB[hs].t(0, (q0, q0 + 512)), True, not has_band)
                        if has_band:
                            mm(pst(pair[m], b0 - q0, b1 - q0), IDENT.t(0),
                               BW.t(h, (b0 - 128 * (kc - 1), b1 - 128 * (kc - 1))), False, True)
                    return pair, b0, b1

                def exps(qb, kc, pair, b0, b1):
                    q0 = qb * 512
                    for m in range(2):
                        P = PT[(kc % 2) * 2 + m]
                        segs = []
                        if b1 > b0:
                            if b0 > q0:
                                segs.append((q0, b0, pv(PV_RBHI + h)))
                            segs.append((b0, b1, None))
                            if b1 < q0 + 512:
                                segs.append((b1, q0 + 512, pv(PV_RBLO + h)))
                        elif 128 * (kc - 1) >= q0 + 512:
                            segs.append((q0, q0 + 512, pv(PV_RBHI + h)))
                        else:
                            segs.append((q0, q0 + 512, pv(PV_RBLO + h)))
                        for (c0, c1, bias) in segs:
                            act(P.t(0, (c0 - q0, c1 - q0)), pst(pair[m], c0 - q0, c1 - q0), AF.Exp, bias=bias)

                def av(kc):
                    for m in range(2):
                        P = PT[(kc % 2) * 2 + m].t(0)
                        mm(psT[4 + m], VALL.t(kc, (h * 128, h * 128 + 128)), P, kc == 0, kc == 15)
                        mm(psT[6 + m], ONES1.t(0), P, kc == 0, kc == 15)

                def end_of_block(qb):
                    qsl = (qb * 512, qb * 512 + 512)
                    act(OT[1].t(0), psT[4], AF.Copy)
                    cp("dve", OT[0].t(0), psT[6])
                    act(OT[3].t(0), psT[5], AF.Copy)
                    cp("dve", OT[2].t(0), psT[7])
                    vop("dve", "reciprocal", OT[0].t(0), [("in_", OT[0].t(0))])
                    vop("dve", "reciprocal", OT[2].t(0), [("in_", OT[2].t(0))])
                    tt("dve", OT[1].t(0), OT[1].t(0), OT[0].t(0), ALU.mult)
                    stt("dve", OT[3].t(0), OT[3].t(0), LAMS.t(0, (5, 6)), OT[2].t(0), ALU.mult, ALU.mult)
                    tt("pool", OT[1].t(0), OT[1].t(0), OT[3].t(0), ALU.add)
                    tt("pool", SQ.t(0), OT[1].t(0), OT[1].t(0), ALU.mult)

                    def subln(bank, h=h, qsl=qsl):
                        mm(psT[bank], ONESV.t(0), SQ.t(0), True, True)
                        act(RSTD.t(0), psT[bank], AF.Ln, bias=SUBLN_EPS)
                        act(RSTD.t(0), RSTD.t(0), AF.Exp, scale=-0.5)
                        stt("dve", OB.t(h, qsl), OT[1].t(0), LAMS.t(0, (6, 7)), RSTD.t(0), ALU.mult, ALU.mult)
                    deferred.append(subln)

                seq = [(qb, kc) for qb in range(NT) for kc in range(16)]
                info = scores(*seq[0])
                for i, (qb, kc) in enumerate(seq):
                    nxt = scores(*seq[i + 1]) if i + 1 < len(seq) else None
                    exps(qb, kc, *info)
                    av(kc)
                    info = nxt
                    if kc == 11:
                        flush_deferred(2)
                    if kc == 15:
                        end_of_block(qb)
            if stop_after == "m2":
                flush_deferred(2)
            tap("o", OB.t(), [128, 4, S], BF16)
            if stop_after == "m2":
                return

            MRG = sbuf(49152, BF16, 8, S)
            WG = [sbuf(16384 + i * 8192, BF16, 16, 128) for i in range(2)]
            WFA = [sbuf(16384 + 4096 + i * 8192, BF16, 8, 128) for i in range(2)]
            SGA = [sbuf(81920 + i * 2048, F32, 1, 512) for i in range(2)]
            SGB = [sbuf(86016 + i * 4096, F32, 1, 512) for i in range(2)]

            def load_m(dc):
                if dc == 0:
                    return
                dma("pool", WG[dc % 2].t(), dr(wgate_d[dc], "wgate"), "wg%d" % (dc % 2))
                dma("pool", WFA[dc % 2].t(), dr(wfa_d[dc], "wfa"), "wfa%d" % (dc % 2))
            WOR = [sbuf(16384 + dc * 2048, BF16, 8, 128) for dc in range(8)]
            load_m(0)
            it = 0
            for dc in range(8):
                if dc + 1 < 8:
                    load_m(dc + 1)
                else:
                    for d2 in range(4):
                        dma("pool", WOR[d2].t(), dr(wout_d[d2], "wout"), "wo%d" % d2)
                for tb in range(NT):
                    tsl = (tb * 512, tb * 512 + 512)
                    bs = (0, 1, 2, 3) if it % 2 == 0 else (4, 5, 6, 7)
                    wg_ = WG0 if dc == 0 else WG[dc % 2]
                    wfa_ = WFA0 if dc == 0 else WFA[dc % 2]
                    for k in range(8):
                        mm(psT[bs[2]], wg_.t(k), H.t(k, tsl), k == 0, k == 7)
                    for k in range(8):
                        mm(psT[bs[3]], wg_.t(8 + k), H.t(k, tsl), k == 0, k == 7)
                    for k in range(4):
                        mm(psT[bs[0]], wfa_.t(k), FB.t(k, tsl), k == 0, k == 3)
                    for k in range(4):
                        mm(psT[bs[1]], wfa_.t(4 + k), OB.t(k, tsl), k == 0, k == 3)
                    act(SGA[it % 2].t(0), psT[bs[2]], AF.Sigmoid)
                    act(SGB[it % 2].t(0), psT[bs[3]], AF.Sigmoid)
                    tt("dve", SGA[it % 2].t(0), SGA[it % 2].t(0), psT[bs[0]], ALU.mult)
                    tt("dve", SGB[it % 2].t(0), SGB[it % 2].t(0), psT[bs[1]], ALU.mult)
                    tt("pool", MRG.t(dc, tsl), SGA[it % 2].t(0), SGB[it % 2].t(0), ALU.add)
                    it += 1
                    if it == 2:
                        flush_deferred(0)
            tap("mrg", MRG.t(), [128, 8, S], BF16)
            for dc in range(4, 8):
                dma("pool", WOR[dc].t(), dr(wout_d[dc], "wout"), "wo%d" % dc)
            if after_merge is not None:
                after_merge()
            pend = []
            for tb in range(NT):
                tsl = (tb * 512, tb * 512 + 512)
                for dc in range(8):
                    for _ in range(2):
                        if pend:
                            pend.pop(0)()
                    b = nb()
                    for k in range(8):
                        mm(psT[b], WOR[dc].t(k), MRG.t(k, tsl), k == 0, k == 7)
                    tt("dve", X.t(dc, tsl), psT[b], X.t(dc, tsl), ALU.add)
                pend.extend(norm_steps([tb], h_out(PV_G2)))
            run_all(pend)

        STAGES = ["ffn1", "m1", "m2", "m3", "ffn2"]
        last = STAGES.index(stop_after) if stop_after else len(STAGES) - 1
        f1_pre, f1_body = make_ffn(wgu_d[0], wd_d[0], "f1")
        f2_pre, f2_body = make_ffn(wgu_d[1], wd_d[1], "f2")
        f1_pre()
        run_all(norm_steps([0, 1], h_out(PV_G1), fast=True))
        if last >= 1:
            as1, as2 = attn_setup_steps()

            def wuf_prefetch():
                WUF = [sbuf(81920 + i * 2048, BF16, 8, 128) for i in range(2)]
                for g in range(2):
                    dma("pool", WUF[g].t(), dr(wuf_d[g], "wuf"), "wuf%d" % g)
            f1_body(pump0=norm_steps([2, 3], h_out(PV_G1)) + as1, pump1=as2 + norm_steps([0, 1], h_out(PV_GM)),
                    after_gu=wuf_prefetch)
            tap("x1", X.t(), [128, 8, S], F32)
            run_all(norm_steps([2, 3], h_out(PV_GM), fast=True))
            mixer(after_merge=f2_pre if last >= 4 else None)
        else:
            f1_body(pump0=norm_steps([2, 3], h_out(PV_G1)))
        if last >= 4:
            tap("x2", X.t(), [128, 8, S], F32)
            f2_body(pump1=norm_steps([0, 1], f_out))
            run_all(norm_steps([2, 3], f_out, fast=True))
        else:
            run_all(norm_steps(range(NT), f_out))

        sc.finalize()
        esem = {k: es.enter_context(nc.semaphore("s_" + k)) for k in ("pe", "act", "dve", "pool", "sp")}
        dsem = {k: es.enter_context(nc.semaphore("d_" + k)) for k in sc.dcnt}
        final_waits = [(dsem[k], v) for k, v in sc.dcnt.items()]
        with nc.Block() as block:
            @block.tensor
            def _(e):
                sc.emit("pe", e, esem, dsem)

            @block.scalar
            def _(e):
                sc.emit("act", e, esem, dsem)

            @block.vector
            def _(e):
                sc.emit("dve", e, esem, dsem)

            @block.gpsimd
            def _(e):
                sc.emit("pool", e, esem, dsem)

            @block.sync
            def _(e):
                sc.emit("sp", e, esem, dsem)
                for sem, v in final_waits:
                    e.wait_ge(sem, v)
    return nc, sc


def _tile_w(w, ncols_chunk):
    K, N = w.shape
    return np.ascontiguousarray(w.reshape(K // 128, 128, N // ncols_chunk, ncols_chunk).transpose(2, 1, 0, 3))


def _dft_consts():
    c = np.arange(128)
    ang = 2.0 * np.pi * np.outer(c, c) / 128.0
    cc = np.concatenate([np.cos(ang), np.sin(ang)], axis=1).astype(ml_dtypes.bfloat16)
    s = np.arange(S, dtype=np.int64)
    prod = np.outer(s, s) % S
    ang = 2.0 * np.pi * prod.astype(np.float64) / S
    sc_ = 1.0 / 512.0
    cs = (np.cos(ang) * sc_).astype(np.float32)
    sn = (-np.sin(ang) * sc_).astype(np.float32)
    out = np.empty((4, 2, 128, 16 * 512), dtype=ml_dtypes.bfloat16)
    for t, M in enumerate((cs, sn)):
        M4 = M.reshape(16, 128, 4, 512).transpose(2, 1, 0, 3)
        out[:, t] = M4.reshape(4, 128, 16 * 512).astype(ml_dtypes.bfloat16)
    return cc, out


_CONSTS = None


def prepare_inputs(inp):
    global _CONSTS
    if _CONSTS is None:
        _CONSTS = _dft_consts()
    cc, slabs = _CONSTS
    f = lambda k: np.asarray(inp[k], dtype=np.float32)
    pos = np.asarray(inp["positions"])
    if not np.array_equal(pos, np.broadcast_to(np.arange(S, dtype=pos.dtype), pos.shape)):
        raise NotImplementedError("kernel supports positions == arange(SEQ) (as produced by setup_inputs)")
    shared = {}
    for i, p in ((1, "ffn1"), (2, "ffn2")):
        wg = _tile_w(f(p + "_wg")[0], 128)
        wu = _tile_w(f(p + "_wu")[0], 128)
        shared["wgu%d" % i] = np.ascontiguousarray(np.stack([wg, wu], axis=2)).reshape(NF, 128, 2 * 8 * 128)
        shared["wd%d" % i] = _tile_w(f(p + "_wd")[0], 128).reshape(8, 128, NF * 128)
    w_in = f("w_in")[0]
    shared["wuf"] = _tile_w(w_in[:, 0:512], 128).reshape(4, 128, 8 * 128)
    shared["wv"] = _tile_w(w_in[:, 1536:2048], 512).reshape(128, 8 * 512)
    wq = _tile_w(w_in[:, 512:1024], 128)
    wk = _tile_w(w_in[:, 1024:1536], 128)
    shared["wqk"] = np.ascontiguousarray(np.stack([wq, wk], axis=2)).reshape(4, 128, 2 * 8 * 128)
    wga = _tile_w(w_in[:, 2048:3072], 128)
    wgb = _tile_w(w_in[:, 3072:4096], 128)
    shared["wgate"] = np.ascontiguousarray(np.stack([wga, wgb], axis=2)).reshape(8, 128, 2 * 8 * 128)
    wfo = _tile_w(f("w_fourier_out")[0], 128)
    wao = _tile_w(f("w_attn_out")[0], 128)
    shared["wfa"] = np.ascontiguousarray(np.stack([wfo, wao], axis=2)).reshape(8, 128, 2 * 4 * 128)
    shared["wout"] = _tile_w(f("w_out")[0], 128).reshape(8, 128, 8 * 128)
    pvec = np.zeros((128, PV_N), np.float32)
    for col, k in ((PV_G1, "ffn1_norm"), (PV_GM, "mix_norm"), (PV_G2, "ffn2_norm"), (PV_GF, "final_norm")):
        pvec[:, col:col + 8] = f(k).reshape(8, 128).T
    pvec[:, PV_SUB] = f("subln_g").reshape(128)
    rb = f("rel_bias")
    pvec[:, PV_RBLO:PV_RBLO + 4] = rb[15][None, :]
    pvec[:, PV_RBHI:PV_RBHI + 4] = rb[31][None, :]
    for j, k in enumerate(("lambda_q1", "lambda_k1", "lambda_q2", "lambda_k2")):
        pvec[:, PV_LAM + 64 * j:PV_LAM + 64 * j + 64] = f(k).reshape(1, 64)
    shared["pvec"] = pvec
    shared["rbrep"] = np.ascontiguousarray(np.repeat(rb[:, :, None], 128, axis=2)).reshape(32, 4 * 128)
    shared["cc"] = cc
    shared["slabs"] = slabs
    x = f("x")
    in_maps = []
    for b in range(x.shape[0]):
        m = dict(shared)
        m["xT"] = np.ascontiguousarray(x[b].T)
        in_maps.append(m)
    return in_maps


_NC = None


def kernel(**inputs):
    global _NC
    in_maps = prepare_inputs(inputs)
    if _NC is None:
        _NC = build_program()[0]
    res = run_bass_kernel_spmd(_NC, in_maps, core_ids=list(range(len(in_maps))))
    out = np.stack([np.ascontiguousarray(r["outT"].T) for r in res.results], axis=0)
    return out.astype(np.float32)
```
